# Optimizing a Trainium2 kernel written in Bass

```python
import jax
import jax.numpy as jnp
from jax import lax
import numpy as np

D_MODEL = 1024
BATCH = 8
SEQ = 4096
DEPTH = 1

N_MEM = 256
ROPE_THETA = 500000.0
NORM_EPS = 1e-6
NEG_INF = -1e30
Q_BLOCK = 128
N_BRANCHES = 3

MLA_HEADS = 8
MLA_Q_RANK = 384
MLA_KV_RANK = 128
MLA_NOPE_DIM = 64
MLA_ROPE_DIM = 32
MLA_V_DIM = 64
MLA_QK_DIM = MLA_NOPE_DIM + MLA_ROPE_DIM

DIL_GROUPS = ((128, 1), (512, 4), (2048, 16))
DIL_HEADS = 4
DIL_HEAD_DIM = 64
DIL_ROPE_DIM = DIL_HEAD_DIM // 4
DIL_WIDTH = len(DIL_GROUPS) * DIL_HEADS * DIL_HEAD_DIM

MEM_HEADS = 4
MEM_HEAD_DIM = 128
MEM_WIDTH = MEM_HEADS * MEM_HEAD_DIM

D_FF = 2816

IN_SPLITS = (MLA_Q_RANK, MLA_KV_RANK, MLA_ROPE_DIM,
             DIL_WIDTH, DIL_WIDTH, DIL_WIDTH,
             MEM_WIDTH, N_BRANCHES * D_MODEL)
D_IN = sum(IN_SPLITS)

kernel_name = 'hybrid_mla_dilated_mem_encoder_block'


def rms_norm(x, g):
    xf = x.astype(jnp.float32)
    y = xf * lax.rsqrt(jnp.mean(xf * xf, axis=-1, keepdims=True) + NORM_EPS)
    return (y * g.astype(jnp.float32)).astype(x.dtype)


def swiglu(h, w_gate, w_up, w_down):
    return (jax.nn.silu(h @ w_gate) * (h @ w_up)) @ w_down


def split_columns(z, sizes):
    outs, start = [], 0
    for n in sizes:
        outs.append(z[..., start:start + n])
        start += n
    return outs


def rope(x, pos, rot_dim):
    half = rot_dim // 2
    inv_freq = ROPE_THETA ** (-2.0 * jnp.arange(half, dtype=jnp.float32) / rot_dim)
    ang = pos.astype(jnp.float32)[..., None] * inv_freq
    cos = jnp.cos(ang)[:, :, None, :]
    sin = jnp.sin(ang)[:, :, None, :]
    xr = x[..., :rot_dim].astype(jnp.float32)
    x1, x2 = xr[..., :half], xr[..., half:]
    rot = jnp.concatenate([x1 * cos - x2 * sin, x2 * cos + x1 * sin], axis=-1).astype(x.dtype)
    return jnp.concatenate([rot, x[..., rot_dim:]], axis=-1)


def dense_block_attention(q, k, v):
    B, S, H, dq = q.shape
    nq = S // Q_BLOCK
    scale = dq ** -0.5
    qb = q.reshape(B, nq, Q_BLOCK, H, dq).transpose(1, 0, 2, 3, 4)

    def attend(q_blk):
        s = jnp.einsum('bqhc,bkhc->bhqk', q_blk, k).astype(jnp.float32) * scale
        p = jax.nn.softmax(s, axis=-1).astype(v.dtype)
        return jnp.einsum('bhqk,bkhc->bqhc', p, v)

    o = lax.map(attend, qb)
    return o.transpose(1, 0, 2, 3, 4).reshape(B, S, H, v.shape[-1])


def dilated_window_attention(q, k, v, dilation, n_side):
    B, S, H, dh = q.shape
    L = S // dilation
    blk = n_side
    nb = -(-L // blk)
    Lp = nb * blk

    def classes(t):
        t = t.reshape(B, L, dilation, H, dh).transpose(0, 2, 1, 3, 4)
        return jnp.pad(t, ((0, 0), (0, 0), (0, Lp - L), (0, 0), (0, 0)))

    def windows(t):
        tp = jnp.pad(t, ((0, 0), (0, 0), (blk, blk), (0, 0), (0, 0)))
        tp = tp.reshape(B, dilation, nb + 2, blk, H, dh)
        return jnp.concatenate([tp[:, :, :-2], tp[:, :, 1:-1], tp[:, :, 2:]], axis=3)

    qb = classes(q).reshape(B, dilation, nb, blk, H, dh)
    kw = windows(classes(k))
    vw = windows(classes(v))

    q_idx = jnp.arange(Lp).reshape(nb, blk)
    k_idx = jnp.arange(nb)[:, None] * blk - blk + jnp.arange(3 * blk)[None, :]
    rel = k_idx[:, None, :] - q_idx[:, :, None]
    mask = (jnp.abs(rel) <= n_side) & (k_idx[:, None, :] >= 0) & (k_idx[:, None, :] < L)

    s = jnp.einsum('bdnqhc,bdnkhc->bdnhqk', qb, kw).astype(jnp.float32) * dh ** -0.5
    s = jnp.where(mask[:, None], s, NEG_INF)
    lse = jax.nn.logsumexp(s, axis=-1)
    p = jnp.exp(s - lse[..., None]).astype(v.dtype)
    o = jnp.einsum('bdnhqk,bdnkhc->bdnqhc', p, vw)
    o = o.reshape(B, dilation, Lp, H, dh)[:, :, :L].transpose(0, 2, 1, 3, 4).reshape(B, S, H, dh)
    lse = lse.transpose(0, 1, 2, 4, 3).reshape(B, dilation, Lp, H)[:, :, :L]
    lse = lse.transpose(0, 2, 1, 3).reshape(B, S, H)
    return o, lse


def mla_branch(c_q, c_kv, k_r, pos, q_norm, w_uq, kv_norm, w_ukv, w_o):
    B, S, _ = c_q.shape
    q = (rms_norm(c_q, q_norm) @ w_uq).reshape(B, S, MLA_HEADS, MLA_QK_DIM)
    q = jnp.concatenate([q[..., :MLA_NOPE_DIM],
                         rope(q[..., MLA_NOPE_DIM:], pos, MLA_ROPE_DIM)], axis=-1)
    kv = (rms_norm(c_kv, kv_norm) @ w_ukv).reshape(B, S, MLA_HEADS, MLA_NOPE_DIM + MLA_V_DIM)
    k_rope = rope(k_r[:, :, None, :], pos, MLA_ROPE_DIM)
    k = jnp.concatenate([kv[..., :MLA_NOPE_DIM],
                         jnp.broadcast_to(k_rope, (B, S, MLA_HEADS, MLA_ROPE_DIM))], axis=-1)
    v = kv[..., MLA_NOPE_DIM:]
    o = dense_block_attention(q, k, v)
    return o.reshape(B, S, MLA_HEADS * MLA_V_DIM) @ w_o


def dilated_branch(q, k, v, pos, w_o):
    B, S, _ = q.shape
    G = len(DIL_GROUPS)

    def heads(t):
        t = t.reshape(B, S, G * DIL_HEADS, DIL_HEAD_DIM)
        return t

    qh = rope(heads(q), pos, DIL_ROPE_DIM).reshape(B, S, G, DIL_HEADS, DIL_HEAD_DIM)
    kh = rope(heads(k), pos, DIL_ROPE_DIM).reshape(B, S, G, DIL_HEADS, DIL_HEAD_DIM)
    vh = v.reshape(B, S, G, DIL_HEADS, DIL_HEAD_DIM)
    outs, lses = [], []
    for g, (window, dilation) in enumerate(DIL_GROUPS):
        n_side = window // (2 * dilation)
        o, lse = dilated_window_attention(qh[:, :, g], kh[:, :, g], vh[:, :, g], dilation, n_side)
        outs.append(o)
        lses.append(lse)
    alpha = jax.nn.softmax(jnp.stack(lses, axis=0), axis=0)
    o = jnp.sum(alpha[..., None] * jnp.stack(outs, axis=0).astype(jnp.float32), axis=0)
    return o.astype(q.dtype).reshape(B, S, DIL_HEADS * DIL_HEAD_DIM) @ w_o


def memory_branch(q, mem, mem_norm, w_kv, w_o):
    B, S, _ = q.shape
    M = mem.shape[1]
    kv = rms_norm(mem, mem_norm) @ w_kv
    km = kv[..., :MEM_WIDTH].reshape(B, M, MEM_HEADS, MEM_HEAD_DIM)
    vm = kv[..., MEM_WIDTH:].reshape(B, M, MEM_HEADS, MEM_HEAD_DIM)
    qh = q.reshape(B, S, MEM_HEADS, MEM_HEAD_DIM)
    s = jnp.einsum('bshc,bmhc->bhsm', qh, km).astype(jnp.float32) * MEM_HEAD_DIM ** -0.5
    p = jax.nn.softmax(s, axis=-1).astype(vm.dtype)
    o = jnp.einsum('bhsm,bmhc->bshc', p, vm)
    return o.reshape(B, S, MEM_WIDTH) @ w_o


def setup_inputs(seed: int = 0) -> dict:
    key = jax.random.key(seed)
    ks = jax.random.split(key, 24)
    f32 = jnp.float32

    def dense(k, shape):
        return jax.random.normal(k, shape, f32) * shape[-2] ** -0.5

    def gain(k, n):
        return 1.0 + 0.02 * jax.random.normal(k, (DEPTH, n), f32)

    offsets = jax.random.randint(ks[2], (BATCH, 1), 0, 1024, dtype=jnp.int32)
    positions = jnp.arange(SEQ, dtype=jnp.int32)[None, :] + offsets
    return {
        'x': jax.random.normal(ks[0], (BATCH, SEQ, D_MODEL), f32),
        'mem': jax.random.normal(ks[1], (BATCH, N_MEM, D_MODEL), f32),
        'positions': positions,
        'ffn1_norm': gain(ks[3], D_MODEL),
        'ffn1_w_gate': dense(ks[4], (DEPTH, D_MODEL, D_FF)),
        'ffn1_w_up': dense(ks[5], (DEPTH, D_MODEL, D_FF)),
        'ffn1_w_down': dense(ks[6], (DEPTH, D_FF, D_MODEL)),
        'mix_norm': gain(ks[7], D_MODEL),
        'w_in': dense(ks[8], (DEPTH, D_MODEL, D_IN)),
        'mla_q_norm': gain(ks[9], MLA_Q_RANK),
        'mla_w_uq': dense(ks[10], (DEPTH, MLA_Q_RANK, MLA_HEADS * MLA_QK_DIM)),
        'mla_kv_norm': gain(ks[11], MLA_KV_RANK),
        'mla_w_ukv': dense(ks[12], (DEPTH, MLA_KV_RANK, MLA_HEADS * (MLA_NOPE_DIM + MLA_V_DIM))),
        'mla_w_o': dense(ks[13], (DEPTH, MLA_HEADS * MLA_V_DIM, D_MODEL)),
        'dil_w_o': dense(ks[14], (DEPTH, DIL_HEADS * DIL_HEAD_DIM, D_MODEL)),
        'mem_norm': gain(ks[15], D_MODEL),
        'mem_w_kv': dense(ks[16], (DEPTH, D_MODEL, 2 * MEM_WIDTH)),
        'mem_w_o': dense(ks[17], (DEPTH, MEM_WIDTH, D_MODEL)),
        'w_out': dense(ks[18], (DEPTH, D_MODEL, D_MODEL)),
        'ffn2_norm': gain(ks[19], D_MODEL),
        'ffn2_w_gate': dense(ks[20], (DEPTH, D_MODEL, D_FF)),
        'ffn2_w_up': dense(ks[21], (DEPTH, D_MODEL, D_FF)),
        'ffn2_w_down': dense(ks[22], (DEPTH, D_FF, D_MODEL)),
        'final_norm': 1.0 + 0.02 * jax.random.normal(ks[23], (D_MODEL,), f32),
    }


def reference(x, mem, positions, ffn1_norm, ffn1_w_gate, ffn1_w_up, ffn1_w_down,
              mix_norm, w_in, mla_q_norm, mla_w_uq, mla_kv_norm, mla_w_ukv, mla_w_o,
              dil_w_o, mem_norm, mem_w_kv, mem_w_o, w_out,
              ffn2_norm, ffn2_w_gate, ffn2_w_up, ffn2_w_down, final_norm):
    B, S, _ = x.shape
    for l in range(DEPTH):
        x = x + 0.5 * swiglu(rms_norm(x, ffn1_norm[l]), ffn1_w_gate[l], ffn1_w_up[l], ffn1_w_down[l])

        h = rms_norm(x, mix_norm[l])
        z = h @ w_in[l]
        c_q, c_kv, k_r, dq, dk, dv, mq, gate_logits = split_columns(z, IN_SPLITS)

        y_mla = mla_branch(c_q, c_kv, k_r, positions, mla_q_norm[l], mla_w_uq[l],
                           mla_kv_norm[l], mla_w_ukv[l], mla_w_o[l])
        y_dil = dilated_branch(dq, dk, dv, positions, dil_w_o[l])
        y_mem = memory_branch(mq, mem, mem_norm[l], mem_w_kv[l], mem_w_o[l])

        gates = jax.nn.sigmoid(gate_logits.reshape(B, S, N_BRANCHES, D_MODEL))
        mixed = gates[:, :, 0] * y_mla + gates[:, :, 1] * y_dil + gates[:, :, 2] * y_mem
        x = x + mixed @ w_out[l]

        x = x + 0.5 * swiglu(rms_norm(x, ffn2_norm[l]), ffn2_w_gate[l], ffn2_w_up[l], ffn2_w_down[l])
    return rms_norm(x, final_norm)
```

```python
import contextlib
import math
import os
import numpy as np
import ml_dtypes
import concourse.bass as bass
import concourse.mybir as mybir
from concourse.bass_utils import run_bass_kernel_spmd

F32 = mybir.dt.float32
BF16 = mybir.dt.bfloat16
I32 = mybir.dt.int32
ALU = mybir.AluOpType
AF = mybir.ActivationFunctionType

D = 1024
DFF = 2816
NMEM = 256
DIN = 6432
EPS = 1e-6
DIL_GROUPS = ((128, 1), (512, 4), (2048, 16))


class Res:
    __slots__ = ("name", "w", "rs")

    def __init__(self, name):
        self.name = name
        self.w = None
        self.rs = []


class Lane:
    def __init__(self, S, name):
        self.sem = S.newsem(name)
        self.n = 0


class Buf:
    def __init__(self, t, name):
        self.t = t
        self.r = Res(name)

    def __getitem__(self, k):
        return self.t[k]


class Sched:
    ENGS = ("pe", "dve", "act", "pool", "sp")

    def __init__(self, nc, stack):
        self.nc = nc
        self.stack = stack
        self.sems = {}
        self.cnt = {}
        self.seen = {k: {} for k in self.ENGS}
        self.prog = {k: [] for k in self.ENGS}
        self.lanes = []
        for k in self.ENGS:
            self.sems[k] = self.newsem("s_" + k)
            self.cnt[k] = 0

    def newsem(self, name):
        return self.stack.enter_context(self.nc.semaphore(name))

    def lane(self, name):
        l = Lane(self, name)
        self.lanes.append(l)
        return l

    def _deps(self, eng, reads, writes):
        deps = {}

        def add(tok, raw):
            if tok is None:
                return
            sem, val, src = tok
            if src == eng:
                if eng == "pe":
                    return
            key = id(sem)
            if key not in deps or deps[key][1] < val:
                deps[key] = (sem, val)

        for r in reads:
            add(r.w, True)
        for w in writes:
            add(w.w, False)
            for t in w.rs:
                add(t, False)
        for key, (sem, val) in deps.items():
            if self.seen[eng].get(key, 0) < val:
                self.prog[eng].append(("w", sem, val))
                self.seen[eng][key] = val

    def _commit(self, tok, reads, writes):
        for r in reads:
            r.rs.append(tok)
            if len(r.rs) > 64:
                best = {}
                for t in r.rs:
                    k = id(t[0])
                    if k not in best or best[k][1] < t[1]:
                        best[k] = t
                r.rs = list(best.values())
        for w in writes:
            w.w = tok
            w.rs = []

    def _skip(self):
        self.total = getattr(self, "total", 0) + 1
        return self.total > int(os.environ.get("KSTOP", "1000000000"))

    def op(self, eng, fn, reads=(), writes=()):
        if self._skip():
            return
        reads = [b.r if isinstance(b, Buf) else b for b in reads]
        writes = [b.r if isinstance(b, Buf) else b for b in writes]
        self._deps(eng, reads, writes)
        self.cnt[eng] += 1
        self.prog[eng].append(("i", fn, self.sems[eng], 1))
        self._commit((self.sems[eng], self.cnt[eng], eng), reads, writes)

    def dma(self, eng, lane, out, in_, reads=(), writes=()):
        if self._skip():
            return
        reads = [b.r if isinstance(b, Buf) else b for b in reads]
        writes = [b.r if isinstance(b, Buf) else b for b in writes]
        self._deps(eng, reads, writes)
        if lane.n > 0 and self.seen[eng].get(id(lane.sem), 0) < 16 * lane.n:
            self.prog[eng].append(("w", lane.sem, 16 * lane.n))
            self.seen[eng][id(lane.sem)] = 16 * lane.n
        lane.n += 1
        self.prog[eng].append(("i", lambda e: e.dma_start(out=out, in_=in_), lane.sem, 16))
        self._commit((lane.sem, 16 * lane.n, "dma"), reads, writes)

    def mark(self, name):
        if os.environ.get("KDEBUG"):
            print("mark", name, getattr(self, "total", 0))

    def barrier(self):
        if os.environ.get("KDEBUG"):
            print("barrier at op count", getattr(self, "total", 0))
        for eng in self.ENGS:
            for k in self.ENGS:
                if k == eng or self.cnt[k] == 0:
                    continue
                key = id(self.sems[k])
                if self.seen[eng].get(key, 0) < self.cnt[k]:
                    self.prog[eng].append(("w", self.sems[k], self.cnt[k]))
                    self.seen[eng][key] = self.cnt[k]
            for l in self.lanes:
                if l.n == 0:
                    continue
                key = id(l.sem)
                if self.seen[eng].get(key, 0) < 16 * l.n:
                    self.prog[eng].append(("w", l.sem, 16 * l.n))
                    self.seen[eng][key] = 16 * l.n

    def emit(self):
        prog = self.prog

        def run(e, lst):
            for it in lst:
                if it[0] == "w":
                    e.wait_ge(it[1], it[2])
                else:
                    it[1](e).then_inc(it[2], it[3])

        with self.nc.Block() as block:
            @block.tensor
            def _(e):
                run(e, prog["pe"])

            @block.vector
            def _(e):
                run(e, prog["dve"])

            @block.scalar
            def _(e):
                run(e, prog["act"])

            @block.gpsimd
            def _(e):
                run(e, prog["pool"])

            @block.sync
            def _(e):
                run(e, prog["sp"])


def dil_mask_list(S):
    out = []
    for g, (win, d) in enumerate(DIL_GROUPS):
        w = 64 * d
        for rel in range(-4096, 4097, 128):
            if rel - 511 <= w and rel + 127 >= -w:
                out.append((g, rel))
    return out


def make_masks():
    ml = dil_mask_list(4096)
    m = np.zeros((len(ml), 128, 512), np.float32)
    kk = np.arange(128)[:, None]
    qq = np.arange(512)[None, :]
    for i, (g, rel) in enumerate(ml):
        d = DIL_GROUPS[g][1]
        delta = rel + kk - qq
        m[i] = ((np.abs(delta) <= 64 * d) & (delta % d == 0)).astype(np.float32)
    return ml, m.astype(ml_dtypes.bfloat16)


WEIGHTS = [
    ("ffn1_w_gate", D, DFF), ("ffn1_w_up", D, DFF), ("ffn1_w_down", DFF, D),
    ("w_in", D, DIN), ("mla_w_uq", 384, 768), ("mla_w_ukv", 128, 1024),
    ("mem_w_kv", D, 1024), ("mla_w_o", 512, D), ("dil_w_o", 256, D), ("mem_w_o", 512, D),
    ("w_out", D, D), ("ffn2_w_gate", D, DFF), ("ffn2_w_up", D, DFF), ("ffn2_w_down", DFF, D),
]
GAINS = [("ffn1_norm", D), ("mix_norm", D), ("mla_q_norm", 384), ("mla_kv_norm", 128),
         ("mem_norm", D), ("ffn2_norm", D), ("final_norm", D)]


def build(S=4096):
    NT = S // 128
    NB = S // 512
    nc = bass.Bass("TRN2", target_bir_lowering=False)
    stack = contextlib.ExitStack()
    Sx = Sched(nc, stack)
    op, dma = Sx.op, Sx.dma

    x_d = nc.dram_tensor("x", [S, D], F32, kind="ExternalInput").ap()
    mem_d = nc.dram_tensor("mem", [NMEM, D], F32, kind="ExternalInput").ap()
    pos_d = nc.dram_tensor("pos", [128, NT], I32, kind="ExternalInput").ap()
    ident_d = nc.dram_tensor("ident", [128, 128], F32, kind="ExternalInput").ap()
    invf_d = nc.dram_tensor("invf", [128, 16], F32, kind="ExternalInput").ap()
    ml, _ = None, None
    mlist = dil_mask_list(4096)
    NM = len(mlist)
    masks_d = nc.dram_tensor("masks", [NM, 128, 512], BF16, kind="ExternalInput").ap()
    masks2_d = nc.dram_tensor("masks2", [2, 128, 512], BF16, kind="ExternalInput").ap()
    mbias_d = nc.dram_tensor("mbias", [NM, 128, 512], BF16, kind="ExternalInput").ap()
    w_d = {n: nc.dram_tensor(n, [k, m], F32, kind="ExternalInput").ap() for n, k, m in WEIGHTS}
    g_d = {n: nc.dram_tensor(n, [1, m], F32, kind="ExternalInput").ap() for n, m in GAINS}
    out_d = nc.dram_tensor("out", [S, D], F32, kind="ExternalOutput").ap()

    wb = {n: nc.dram_tensor(n + "_bf", [k, m], BF16, kind="Internal").ap() for n, k, m in WEIGHTS}
    wb_r = {n: Res(n + "_bf") for n, _, _ in WEIGHTS}
    x1s = nc.dram_tensor("x1s", [S, D], F32, kind="Internal").ap()
    qTs = nc.dram_tensor("qTs", [8, 96, S], BF16, kind="Internal").ap()
    kTs = nc.dram_tensor("kTs", [8, 96, S], BF16, kind="Internal").ap()
    vs = nc.dram_tensor("vs", [S, 8, 64], BF16, kind="Internal").ap()
    dqTs = nc.dram_tensor("dqTs", [6, 128, S], BF16, kind="Internal").ap()
    dkTs = nc.dram_tensor("dkTs", [6, 128, S], BF16, kind="Internal").ap()
    dvs = nc.dram_tensor("dvs", [S, 12, 64], BF16, kind="Internal").ap()
    mqTs = nc.dram_tensor("mqTs", [4, 128, S], BF16, kind="Internal").ap()
    gTs = nc.dram_tensor("gTs", [24, 128, S], BF16, kind="Internal").ap()
    omlaTs = nc.dram_tensor("omlaTs", [8, 64, S], BF16, kind="Internal").ap()
    odilTs = nc.dram_tensor("odilTs", [4, 64, S], BF16, kind="Internal").ap()
    omemTs = nc.dram_tensor("omemTs", [4, 128, S], BF16, kind="Internal").ap()
    R_x1s, R_qTs, R_kTs, R_vs = Res("x1s"), Res("qTs"), Res("kTs"), Res("vs")
    R_dqTs, R_dkTs, R_dvs, R_mqTs, R_gTs = Res("dqTs"), Res("dkTs"), Res("dvs"), Res("mqTs"), Res("gTs")
    R_omla, R_odil, R_omem, R_out = Res("omla"), Res("odil"), Res("omem"), Res("out")

    def sb(name, shape, dt):
        return Buf(nc.alloc_sbuf_tensor(name, shape, dt), name)

    def ps(name, shape, dt=F32):
        return Buf(nc.alloc_psum_tensor(name, shape, dt), name)

    class Pool:
        def __init__(self, bufs):
            self.bufs = bufs
            self.i = 0

        def get(self):
            b = self.bufs[self.i % len(self.bufs)]
            self.i += 1
            return b

    identf = sb("identf", [128, 128], F32)
    identb = sb("identb", [128, 128], BF16)
    onesb = sb("onesb", [128, 128], BF16)
    onesf = sb("onesf", [128, 64], F32)
    gbc = {}
    GDIM = dict(GAINS)

    def load_gains(ctx, names):
        for n in names:
            gbc[n] = Buf(ctx.enter_context(nc.sbuf_tensor("g_" + n, [128, GDIM[n]], F32)), "g_" + n)
            dma("sp", l_c, gbc[n][:], g_d[n].partition_broadcast(128), writes=[gbc[n]])
    cosT = sb("cosT", [128, NT, 16], F32)
    sinT = sb("sinT", [128, NT, 16], F32)
    cosTd = sb("cosTd", [128, NT, 8], F32)
    sinTd = sb("sinTd", [128, NT, 8], F32)
    l_c = Sx.lane("l_const")
    dma("sp", l_c, identf[:], ident_d, writes=[identf])
    op("dve", lambda e: e.tensor_copy(out=identb[:], in_=identf[:]), [identf], [identb])
    op("dve", lambda e: e.memset(onesb[:], 1.0), [], [onesb])
    op("dve", lambda e: e.memset(onesf[:], 1.0), [], [onesf])

    PS = {}

    def psum(ctx, name, shape, dt=F32):
        return Buf(ctx.enter_context(nc.psum_tensor(name, shape, dt)), name)

    def set_psum(ctx, tag, A=0, O=0, W=0, T=0, F=0):
        PS["F"] = Pool([psum(ctx, "psF%s%d" % (tag, i), [128, 512]) for i in range(F)])
        PS["A"] = Pool([psum(ctx, "psA%s%d" % (tag, i), [128, 512]) for i in range(A)])
        PS["O"] = Pool([psum(ctx, "psO%s%d" % (tag, i), [128, 512]) for i in range(O)])
        PS["W"] = Pool([psum(ctx, "psW%s%d" % (tag, i), [128, 1024]) for i in range(W)])
        PS["T"] = Pool([psum(ctx, "psT%s%d" % (tag, i), [128, 8, 128], BF16) for i in range(T)])

    with nc.sbuf_tensor("posi", [128, NT], I32) as posi_t, nc.sbuf_tensor("posf", [128, NT], F32) as posf_t, \
            nc.sbuf_tensor("invf_sb", [128, 16], F32) as invf_t, nc.sbuf_tensor("ang", [128, NT, 16], F32) as ang_t, \
            nc.sbuf_tensor("kf", [128, NT, 16], F32) as kf_t, nc.sbuf_tensor("ki", [128, NT, 16], I32) as ki_t, \
            nc.sbuf_tensor("rr", [128, NT, 16], F32) as rr_t, nc.sbuf_tensor("fx", [128, NT, 16], F32) as fx_t:
        posi, posf, invf, ang, kf, ki, rr, fx = [Buf(t, n) for t, n in (
            (posi_t, "posi"), (posf_t, "posf"), (invf_t, "invf"), (ang_t, "ang"), (kf_t, "kf"), (ki_t, "ki"), (rr_t, "rr"), (fx_t, "fx"))]
        dma("sp", l_c, posi[:], pos_d, writes=[posi])
        dma("sp", l_c, invf[:], invf_d, writes=[invf])
        op("dve", lambda e: e.tensor_copy(out=posf[:], in_=posi[:]), [posi], [posf])
        for t in range(NT):
            op("dve", lambda e, t=t: e.tensor_scalar(out=ang[:, t, :], in0=invf[:], scalar1=posf[:, t:t + 1], scalar2=None, op0=ALU.mult),
               [invf, posf], [ang])
        TWO_PI = 2.0 * math.pi
        C1 = 6.28125
        C2 = TWO_PI - C1
        for (dst, shift) in ((sinT, 0.0), (cosT, math.pi / 2)):
            op("dve", lambda e, shift=shift: e.tensor_scalar(out=rr[:], in0=ang[:], scalar1=shift, scalar2=None, op0=ALU.add), [ang], [rr])
            op("dve", lambda e: e.tensor_scalar(out=kf[:], in0=rr[:], scalar1=1.0 / TWO_PI, scalar2=None, op0=ALU.mult), [rr], [kf])
            op("dve", lambda e: e.tensor_copy(out=ki[:], in_=kf[:]), [kf], [ki])
            op("dve", lambda e: e.tensor_copy(out=kf[:], in_=ki[:]), [ki], [kf])
            op("dve", lambda e: e.scalar_tensor_tensor(out=rr[:], in0=kf[:], scalar=-C1, in1=rr[:], op0=ALU.mult, op1=ALU.add), [kf, rr], [rr])
            op("dve", lambda e: e.scalar_tensor_tensor(out=rr[:], in0=kf[:], scalar=-C2, in1=rr[:], op0=ALU.mult, op1=ALU.add), [kf, rr], [rr])
            op("dve", lambda e: e.tensor_scalar(out=fx[:], in0=rr[:], scalar1=math.pi, scalar2=-TWO_PI, op0=ALU.is_gt, op1=ALU.mult), [rr], [fx])
            op("dve", lambda e: e.tensor_tensor(out=rr[:], in0=rr[:], in1=fx[:], op=ALU.add), [rr, fx], [rr])
            op("dve", lambda e: e.tensor_scalar(out=fx[:], in0=rr[:], scalar1=-math.pi, scalar2=TWO_PI, op0=ALU.is_lt, op1=ALU.mult), [rr], [fx])
            op("dve", lambda e: e.tensor_tensor(out=rr[:], in0=rr[:], in1=fx[:], op=ALU.add), [rr, fx], [rr])
            op("dve", lambda e: e.tensor_scalar(out=rr[:], in0=rr[:], scalar1=math.pi, scalar2=-math.pi, op0=ALU.min, op1=ALU.max), [rr], [rr])
            op("act", lambda e, dst=dst: e.activation(out=dst[:], in_=rr[:], func=AF.Sin), [rr], [dst])
        op("dve", lambda e: e.tensor_copy(out=cosTd[:], in_=cosT[:, :, 0:16:2]), [cosT], [cosTd])
        op("dve", lambda e: e.tensor_copy(out=sinTd[:], in_=sinT[:, :, 0:16:2]), [sinT], [sinTd])
        Sx.barrier()

    CW = 4096
    WDIM = {n: (K, M) for n, K, M in WEIGHTS}
    PRO_W = ["ffn1_w_gate", "ffn1_w_up", "ffn1_w_down", "w_in", "mla_w_uq", "mla_w_ukv", "mem_w_kv"]
    BG_W = [n for n, _, _ in WEIGHTS if n not in PRO_W]

    def cast_pieces(names):
        out = []
        for n in names:
            K, M = WDIM[n]
            for r0 in range(0, K, 128):
                rows = min(128, K - r0)
                for c0 in range(0, M, CW):
                    out.append((n, r0, rows, c0, min(CW, M - c0)))
        return out

    class Caster:
        def __init__(self, ctx, tag, nslots, engs):
            self.f = [Buf(ctx.enter_context(nc.sbuf_tensor("cst_f%s%d" % (tag, i), [128, CW], F32)), "cf%s%d" % (tag, i)) for i in range(nslots)]
            self.b = [Buf(ctx.enter_context(nc.sbuf_tensor("cst_b%s%d" % (tag, i), [128, CW], BF16)), "cb%s%d" % (tag, i)) for i in range(nslots)]
            self.lin = [Sx.lane("l_cin%s%d" % (tag, i)) for i in range(nslots)]
            self.lout = [Sx.lane("l_cout%s%d" % (tag, i)) for i in range(nslots)]
            self.engs = engs
            self.i = 0

        def emit(self, piece):
            n, r0, rows, c0, cw = piece
            s_ = self.i % len(self.f)
            self.i += 1
            f, b = self.f[s_], self.b[s_]
            dma("sp", self.lin[s_], f[0:rows, 0:cw], w_d[n][r0:r0 + rows, c0:c0 + cw], writes=[f])
            eng = self.engs[s_ % len(self.engs)]
            if eng == "act":
                op("act", lambda e: e.activation(out=b[0:rows, 0:cw], in_=f[0:rows, 0:cw], func=AF.Copy), [f], [b])
            else:
                op(eng, lambda e: e.tensor_copy(out=b[0:rows, 0:cw], in_=f[0:rows, 0:cw]), [f], [b])
            dma("pool", self.lout[s_], wb[n][r0:r0 + rows, c0:c0 + cw], b[0:rows, 0:cw], reads=[b], writes=[wb_r[n]])

    with contextlib.ExitStack() as pcast:
        cst = Caster(pcast, "p", 4, ["dve", "act"])
        for piece in cast_pieces(PRO_W):
            cst.emit(piece)
        Sx.barrier()

    ss_pool = Pool([sb("ss%d" % i, [128, 1], F32) for i in range(4)])
    sq_junk = sb("sq_junk", [128, 1024], BF16)
    hb_pool = Pool([sb("hb%d" % i, [128, 1024], BF16) for i in range(2)])

    def rstd_of(src_ap, src_bufs, Dn):
        ss = ss_pool.get()
        inv = 1.0 / math.sqrt(Dn)
        op("act", lambda e: e.activation(out=sq_junk[:, 0:Dn], in_=src_ap, func=AF.Square, scale=inv, accum_out=ss[:]), src_bufs, [sq_junk, ss])
        op("dve", lambda e: e.tensor_scalar(out=ss[:], in0=ss[:], scalar1=EPS, scalar2=None, op0=ALU.add), [ss], [ss])
        op("act", lambda e: e.activation(out=ss[:], in_=ss[:], func=AF.Sqrt), [ss], [ss])
        op("dve", lambda e: e.reciprocal(out=ss[:], in_=ss[:]), [ss], [ss])
        return ss

    def rmsnorm_T(src_ap, src_bufs, Dn, gain, dst, dst_ap_fn):
        ss = rstd_of(src_ap, src_bufs, Dn)
        hb = hb_pool.get()
        op("dve", lambda e: e.scalar_tensor_tensor(out=hb[:, 0:Dn], in0=src_ap, scalar=ss[:, 0:1], in1=gain[:, 0:Dn], op0=ALU.mult, op1=ALU.mult),
           list(src_bufs) + [ss, gain], [hb])
        nk = Dn // 128
        pt = PS["T"].get()
        for kc in range(nk):
            op("pe", lambda e, kc=kc: e.transpose(out=pt[:, kc, :], in_=hb[:, kc * 128:(kc + 1) * 128], identity=identb[:]), [hb, identb], [pt])
        dst_ap = dst_ap_fn(nk)
        op("dve", lambda e: e.tensor_copy(out=dst_ap, in_=pt[:, 0:nk, :]), [pt], [dst])

    lw = [Sx.lane("l_w%d" % i) for i in range(4)]
    wctr = [0]
    FB = {}

    def alloc_ffn(ctx, tag, nslots, gcols):
        def a(name, shape, dt):
            return Buf(ctx.enter_context(nc.sbuf_tensor(name + tag, shape, dt)), name + tag)
        FB["wslots"] = [a("wslot%d" % i, [128, 8, gcols], BF16) for i in range(nslots)]
        FB["gcols"] = gcols
        FB["xb"] = a("xb", [128, 4, 1024], F32)
        FB["hT"] = a("hT", [128, 8, 512], BF16)
        FB["aT"] = a("aT", [128, 22, 512], BF16)
        FB["wd"] = a("wd", [128, 22, 1024], BF16)
        FB["sg"] = Pool([a("sg%d" % i, [128, 512], F32) for i in range(2)])

    def load_w512(name, K, c0, ncols):
        ns = len(FB["wslots"])
        s = wctr[0] % ns
        wctr[0] += 1
        slot = FB["wslots"][s]
        nk = K // 128
        dma("sp", lw[s], slot[:, 0:nk, 0:ncols], wb[name][:, c0:c0 + ncols].rearrange("(kc p) n -> p kc n", p=128),
            reads=[wb_r[name]], writes=[slot])
        return slot

    l_x = Sx.lane("l_x")
    l_x1 = Sx.lane("l_x1")
    l_wd = Sx.lane("l_wd")

    def ffn(pref, gain, xb=None):
        hT, aT, wd, sg_pool = FB["hT"], FB["aT"], FB["wd"], FB["sg"]
        if xb is None:
            xb = FB["xb"]
        G = FB["gcols"]
        for i in range(4):
            rmsnorm_T(xb[:, i, :], [xb], D, gain, hT, lambda nk, i=i: hT[:, 0:nk, i * 128:(i + 1) * 128])
        for c0 in range(0, DFF, G):
            ncols = min(G, DFF - c0)
            wg = load_w512(pref + "_w_gate", D, c0, ncols)
            wu = load_w512(pref + "_w_up", D, c0, ncols)
            for cc in range(ncols // 128):
                c = c0 // 128 + cc
                pg = PS["A"].get()
                for kc in range(8):
                    op("pe", lambda e, kc=kc, cc=cc, pg=pg, wg=wg: e.matmul(pg[:], lhsT=wg[:, kc, cc * 128:(cc + 1) * 128], rhs=hT[:, kc, :], start=(kc == 0), stop=(kc == 7)),
                       [wg, hT], [pg])
                pu = PS["A"].get()
                for kc in range(8):
                    op("pe", lambda e, kc=kc, cc=cc, pu=pu, wu=wu: e.matmul(pu[:], lhsT=wu[:, kc, cc * 128:(cc + 1) * 128], rhs=hT[:, kc, :], start=(kc == 0), stop=(kc == 7)),
                       [wu, hT], [pu])
                sg = sg_pool.get()
                op("act", lambda e, sg=sg, pg=pg: e.activation(out=sg[:], in_=pg[:], func=AF.Silu), [pg], [sg])
                op("dve", lambda e, sg=sg, pu=pu, c=c: e.tensor_tensor(out=aT[:, c, :], in0=sg[:], in1=pu[:], op=ALU.mult), [sg, pu], [aT])
        for half in range(2):
            for i in range(4):
                pd = PS["A"].get()
                for c in range(22):
                    op("pe", lambda e, c=c, i=i, half=half, pd=pd: e.matmul(pd[:], lhsT=aT[:, c, i * 128:(i + 1) * 128], rhs=wd[:, c, half * 512:(half + 1) * 512], start=(c == 0), stop=(c == 21)),
                       [aT, wd], [pd])
                op("dve", lambda e, i=i, half=half, pd=pd: e.scalar_tensor_tensor(out=xb[:, i, half * 512:(half + 1) * 512], in0=pd[:], scalar=0.5, in1=xb[:, i, half * 512:(half + 1) * 512], op0=ALU.mult, op1=ALU.add),
                   [pd, xb], [xb])

    def load_wd(pref):
        wd = FB["wd"]
        dma("sp", l_wd, wd[:], wb[pref + "_w_down"].rearrange("(c p) n -> p c n", p=128), reads=[wb_r[pref + "_w_down"]], writes=[wd])

    with contextlib.ExitStack() as pa:
        def sba(name, shape, dt):
            return Buf(pa.enter_context(nc.sbuf_tensor(name, shape, dt)), name)

        print("SBUF remaining at phase A start", nc.sbuf_bytes_remaining)
        set_psum(pa, "a", A=3, W=1, T=2, F=1)
        load_gains(pa, ["ffn1_norm", "mix_norm", "mla_q_norm", "mla_kv_norm"])
        alloc_ffn(pa, "_a", 3, 512)
        print("SBUF remaining after ffn alloc", nc.sbuf_bytes_remaining)
        xb, hT = FB["xb"], FB["hT"]
        wuq = sba("wuq", [128, 3, 768], BF16)
        wukv = sba("wukv", [128, 1024], BF16)
        def make_bs(tag):
            return dict(
                q_tok=sba("q_tok" + tag, [128, 8, 96], BF16), k_tok=sba("k_tok" + tag, [128, 8, 96], BF16),
                t1=sba("t1" + tag, [128, 128], F32), t2=sba("t2" + tag, [128, 128], F32),
                xs=sba("xs" + tag, [128, 256], F32),
                qTt=sba("qTt" + tag, [128, 8, 128], BF16), kTt=sba("kTt" + tag, [128, 8, 128], BF16),
                vt=sba("vt" + tag, [128, 512], BF16),
                lq=Sx.lane("l_stq" + tag), lk=Sx.lane("l_stk" + tag), lv=Sx.lane("l_stv" + tag))

        BSETS = [make_bs("0"), make_bs("1")]
        gq_fm = sba("gq_fm", [128, 4], F32)
        l_gfm = Sx.lane("l_gfm")
        for kc in range(3):
            dma("sp", l_gfm, gq_fm[:, kc:kc + 1], g_d["mla_q_norm"][0:1, kc * 128:(kc + 1) * 128].rearrange("o p -> p o"), writes=[gq_fm])
        dma("sp", l_gfm, gq_fm[:, 3:4], g_d["mla_kv_norm"][0:1, 0:128].rearrange("o p -> p o"), writes=[gq_fm])
        cqTb = sba("cqTb", [128, 4, 512], BF16)
        sqb = cqTb
        rbc = FB["sg"].bufs
        wkr_buf = sba("wkr_buf", [128, 8, 32], BF16)
        l_wkr = Sx.lane("l_wkr")
        dq_tok = sba("dq_tok", [128, 12, 64], BF16)
        dk_tok = sba("dk_tok", [128, 12, 64], BF16)
        dqTb = sba("dqTb", [128, 6, 512], BF16)
        dkTb = sba("dkTb", [128, 6, 512], BF16)
        dvb = sba("dvb", [128, 768], BF16)
        fst_pool = Pool([sba("fst%d" % i, [128, 2, 512], BF16) for i in range(2)])
        fm_slots = [sba("fmslot%d" % i, [128, 8, 256], BF16) for i in range(2)]
        l_fm = [Sx.lane("l_fm%d" % i) for i in range(2)]
        fm_ctr = [0]
        fm_banks = Pool(PS["F"].bufs + PS["A"].bufs)
        print("SBUF remaining after phase A alloc", nc.sbuf_bytes_remaining)

        def fm_gen(blk):
            tok0 = blk * 512
            for gi in range(14):
                c0 = 2848 + gi * 256
                k_ = fm_ctr[0] % 2
                fm_ctr[0] += 1
                wsl = fm_slots[k_]
                dma("sp", l_fm[k_], wsl[:], wb["w_in"][:, c0:c0 + 256].rearrange("(kc p) n -> p kc n", p=128), reads=[wb_r["w_in"]], writes=[wsl])
                fi = fst_pool.i % 2
                fst = fst_pool.get()
                for cc in range(2):
                    pf = fm_banks.get()
                    for kc in range(8):
                        op("pe", lambda e, kc=kc, cc=cc, pf=pf, wsl=wsl: e.matmul(pf[:], lhsT=wsl[:, kc, cc * 128:(cc + 1) * 128], rhs=hT[:, kc, :], start=(kc == 0), stop=(kc == 7)), [wsl, hT], [pf])
                    if gi < 2:
                        op("dve", lambda e, pf=pf, fst=fst, cc=cc: e.tensor_copy(out=fst[:, cc, :], in_=pf[:]), [pf], [fst])
                    else:
                        op("act", lambda e, pf=pf, fst=fst, cc=cc: e.activation(out=fst[:, cc, :], in_=pf[:], func=AF.Sigmoid), [pf], [fst])
                    if cc == 1:
                        if gi < 2:
                            dma("pool", l_fst[fi], mqTs[gi * 2:gi * 2 + 2, :, tok0:tok0 + 512].rearrange("h p t -> p h t"), fst[:], reads=[fst], writes=[R_mqTs])
                        else:
                            cch = (gi - 2) * 2
                            dma("pool", l_fst[fi], gTs[cch:cch + 2, :, tok0:tok0 + 512].rearrange("h p t -> p h t"), fst[:], reads=[fst], writes=[R_gTs])
                    yield
        l_a = Sx.lane("l_a_misc")
        l_st = {k: Sx.lane("l_st_" + k) for k in ("q", "k", "v", "dq", "dk", "dv")}
        l_fst = [Sx.lane("l_fst%d" % i) for i in range(2)]
        dma("sp", l_a, wuq[:], wb["mla_w_uq"].rearrange("(kc p) n -> p kc n", p=128), reads=[wb_r["mla_w_uq"]], writes=[wuq])
        dma("sp", l_a, wukv[:], wb["mla_w_ukv"], reads=[wb_r["mla_w_ukv"]], writes=[wukv])
        load_wd("ffn1")

        def rope(src3, nh, half, off, cs_t, dst3, t):
            cos_ap = cosT[:, t:t + 1, 0:16:(16 // half)].to_broadcast([128, nh, half])
            sin_ap = sinT[:, t:t + 1, 0:16:(16 // half)].to_broadcast([128, nh, half])
            return cos_ap, sin_ap


        def build_tables(t, bs):
            for (src, dM, dD) in ((cosT, bs["cosM"], bs["cosD"]), (sinT, bs["sinM"], bs["sinD"])):
                op("pool", lambda e, src=src, dM=dM: e.tensor_copy(out=dM[:, 0, :], in_=src[:, t, :]), [src], [dM])
                for (a0, n) in ((1, 1), (2, 2), (4, 4)):
                    op("pool", lambda e, dM=dM, a0=a0, n=n: e.tensor_copy(out=dM[:, a0:a0 + n, :], in_=dM[:, 0:n, :]), [dM], [dM])
                op("pool", lambda e, src=src, dD=dD: e.tensor_copy(out=dD[:, 0, :], in_=src[:, t, 0:16:2]), [src], [dD])
                for (a0, n) in ((1, 1), (2, 2), (4, 4), (8, 4)):
                    op("pool", lambda e, dD=dD, a0=a0, n=n: e.tensor_copy(out=dD[:, a0:a0 + n, :], in_=dD[:, 0:n, :]), [dD], [dD])

        def rope_apply(x_view, x_bufs, nh, half, t, dst_view, dst_buf, bs):
            xs, t1, t2 = bs["xs"], bs["t1"], bs["t2"]
            if half == 16:
                cb, sb_ = cosT, sinT
            else:
                cb, sb_ = cosTd, sinTd
            cos_ap = cb[:, t:t + 1, :].to_broadcast([128, nh, half])
            sin_ap = sb_[:, t:t + 1, :].to_broadcast([128, nh, half])
            xsv = xs[:, 0:nh * 2 * half].rearrange("p (h c) -> p h c", h=nh)
            op("act", lambda e: e.activation(out=xsv, in_=x_view(0, 2 * half), func=AF.Copy), list(x_bufs), [xs])
            x1 = xsv[:, :, 0:half]
            x2 = xsv[:, :, half:2 * half]
            a = t1[:, 0:nh * half].rearrange("p (h c) -> p h c", h=nh)
            b = t2[:, 0:nh * half].rearrange("p (h c) -> p h c", h=nh)
            op("dve", lambda e: e.tensor_tensor(out=a, in0=cos_ap, in1=x1, op=ALU.mult), [xs, cb], [t1])
            op("dve", lambda e: e.tensor_tensor(out=b, in0=sin_ap, in1=x2, op=ALU.mult), [xs, sb_], [t2])
            op("dve", lambda e: e.tensor_tensor(out=dst_view(0, half), in0=a, in1=b, op=ALU.subtract), [t1, t2], [dst_buf])
            op("dve", lambda e: e.tensor_tensor(out=a, in0=cos_ap, in1=x2, op=ALU.mult), [xs, cb], [t1])
            op("dve", lambda e: e.tensor_tensor(out=b, in0=sin_ap, in1=x1, op=ALU.mult), [xs, sb_], [t2])
            op("dve", lambda e: e.tensor_tensor(out=dst_view(half, 2 * half), in0=a, in1=b, op=ALU.add), [t1, t2], [dst_buf])

        def norm1(src_ap, src_bufs, Dn, gain, hb):
            ss = rstd_of(src_ap, src_bufs, Dn)
            op("dve", lambda e: e.scalar_tensor_tensor(out=hb[:, 0:Dn], in0=src_ap, scalar=ss[:, 0:1], in1=gain[:, 0:Dn], op0=ALU.mult, op1=ALU.mult),
               list(src_bufs) + [ss, gain], [hb])

        def norm2(hb, Dn, dst):
            nk = Dn // 128
            pt = PS["T"].get()
            for kc in range(nk):
                op("pe", lambda e, kc=kc, pt=pt: e.transpose(out=pt[:, kc, :], in_=hb[:, kc * 128:(kc + 1) * 128], identity=identb[:]), [hb, identb], [pt])
            op("dve", lambda e, pt=pt: e.tensor_copy(out=dst[:, 0:nk, :], in_=pt[:, 0:nk, :]), [pt], [dst])

        ga_banks = Pool(PS["F"].bufs + PS["A"].bufs)

        def ga_tile(blk, i, wA):
            bs = BSETS[i % 2]
            t = blk * 4 + i
            tokA = blk * 512 + i * 128
            q_tok, k_tok = bs["q_tok"], bs["k_tok"]
            pK = ga_banks.get()
            for kc in range(8):
                op("pe", lambda e, kc=kc, pK=pK: e.matmul(pK[:, 0:32], lhsT=hT[:, kc, i * 128:(i + 1) * 128], rhs=wkr_buf[:, kc, 0:32], start=(kc == 0), stop=(kc == 7)), [hT, wkr_buf], [pK])
            yield
            pQ = PS["W"].get()
            for (n0, n1) in ((0, 512), (512, 768)):
                for kc in range(3):
                    op("pe", lambda e, kc=kc, n0=n0, n1=n1, pQ=pQ: e.matmul(pQ[:, n0:n1], lhsT=cqTb[:, kc, i * 128:(i + 1) * 128], rhs=wuq[:, kc, n0:n1], start=(kc == 0), stop=(kc == 2)), [cqTb, wuq], [pQ])
            pQv = pQ[:, 0:768].rearrange("p (h c) -> p h c", h=8)
            op("act", lambda e: e.activation(out=q_tok[:, :, 0:64], in_=pQv[:, :, 0:64], func=AF.Copy), [pQ], [q_tok])
            rope_apply(lambda lo, hi: pQv[:, :, 64 + lo:64 + hi], [pQ], 8, 16, t, lambda lo, hi: q_tok[:, :, 64 + lo:64 + hi], q_tok, bs)
            pKV = PS["W"].get()
            for (n0, n1) in ((0, 512), (512, 1024)):
                op("pe", lambda e, n0=n0, n1=n1, pKV=pKV: e.matmul(pKV[:, n0:n1], lhsT=cqTb[:, 3, i * 128:(i + 1) * 128], rhs=wukv[:, n0:n1], start=True, stop=True), [cqTb, wukv], [pKV])
            pKVv = pKV[:].rearrange("p (h c) -> p h c", h=8)
            vt = bs["vt"]
            op("act", lambda e: e.activation(out=k_tok[:, :, 0:64], in_=pKVv[:, :, 0:64], func=AF.Copy), [pKV], [k_tok])
            op("act", lambda e: e.activation(out=vt[:].rearrange("p (h c) -> p h c", h=8), in_=pKVv[:, :, 64:128], func=AF.Copy), [pKV], [vt])
            pKv = pK[:, 0:32].rearrange("p (h c) -> p h c", h=1)
            rope_apply(lambda lo, hi: pKv[:, :, lo:hi], [pK], 1, 16, t, lambda lo, hi: k_tok[:, 0:1, 64 + lo:64 + hi], k_tok, bs)
            for (a0, n) in ((1, 1), (2, 2), (4, 4)):
                op("pool", lambda e, a0=a0, n=n: e.tensor_copy(out=k_tok[:, a0:a0 + n, 64:96], in_=k_tok[:, 0:n, 64:96]), [k_tok], [k_tok])
            dma("pool", bs["lv"], vs[tokA:tokA + 128, :, :].rearrange("p h c -> p (h c)"), vt[:], reads=[vt], writes=[R_vs])
            yield
            for (src, dstb, dram, lane, rres) in ((q_tok, bs["qTt"], qTs, bs["lq"], R_qTs), (k_tok, bs["kTt"], kTs, bs["lk"], R_kTs)):
                pt = PS["T"].get()
                for h in range(8):
                    op("pe", lambda e, h=h, src=src, pt=pt: e.transpose(out=pt[0:96, h, :], in_=src[:, h, :], identity=identb[:]), [src, identb], [pt])
                op("dve", lambda e, pt=pt, dstb=dstb: e.tensor_copy(out=dstb[0:96, :, :], in_=pt[0:96, :, :]), [pt], [dstb])
                dma("pool", lane, dram[:, :, tokA:tokA + 128].rearrange("h p t -> p h t"), dstb[0:96, :, :], reads=[dstb], writes=[rres])

        for blk in range(NB):
            tok0 = blk * 512
            dma("sp", l_x, xb[:], x_d[tok0:tok0 + 512, :].rearrange("(i p) d -> p i d", p=128), writes=[xb])
            ffn("ffn1", gbc["ffn1_norm"])
            Sx.mark("ffn1 done blk%d" % blk)
            dma("pool", l_x1, x1s[tok0:tok0 + 512, :].rearrange("(i p) d -> p i d", p=128), xb[:], reads=[xb], writes=[R_x1s])
            for i in range(4):
                rmsnorm_T(xb[:, i, :], [xb], D, gbc["mix_norm"], hT, lambda nk, i=i: hT[:, 0:nk, i * 128:(i + 1) * 128])
            wA = load_w512("w_in", D, 0, 512)
            dma("sp", l_wkr, wkr_buf[:], wb["w_in"][:, 512:544].rearrange("(kc p) n -> p kc n", p=128), reads=[wb_r["w_in"]], writes=[wkr_buf])
            pcs = []
            for cc in range(4):
                pc = ga_banks.get()
                pcs.append(pc)
                for kc in range(8):
                    op("pe", lambda e, kc=kc, cc=cc, pc=pc, wA=wA: e.matmul(pc[:], lhsT=wA[:, kc, cc * 128:(cc + 1) * 128], rhs=hT[:, kc, :], start=(kc == 0), stop=(kc == 7)), [wA, hT], [pc])
                op("act", lambda e, cc=cc, pc=pc: e.activation(out=sqb[:, cc, :], in_=pc[:], func=AF.Square), [pc], [sqb])
            pSS = PS["W"].get()
            for kc in range(3):
                op("pe", lambda e, kc=kc, pSS=pSS: e.matmul(pSS[:, 0:512], lhsT=onesb[:], rhs=sqb[:, kc, :], start=(kc == 0), stop=(kc == 2)), [onesb, sqb], [pSS])
            op("pe", lambda e, pSS=pSS: e.matmul(pSS[:, 512:1024], lhsT=onesb[:], rhs=sqb[:, 3, :], start=True, stop=True), [onesb, sqb], [pSS])
            for (j_, dn) in ((0, 384.0), (1, 128.0)):
                r_ = rbc[j_]
                op("dve", lambda e, j_=j_, dn=dn, r_=r_, pSS=pSS: e.tensor_scalar(out=r_[:], in0=pSS[:, j_ * 512:(j_ + 1) * 512], scalar1=1.0 / dn, scalar2=EPS, op0=ALU.mult, op1=ALU.add), [pSS], [r_])
                op("act", lambda e, r_=r_: e.activation(out=r_[:], in_=r_[:], func=AF.Sqrt), [r_], [r_])
                op("dve", lambda e, r_=r_: e.reciprocal(out=r_[:], in_=r_[:]), [r_], [r_])
            for cc in range(4):
                r_ = rbc[0] if cc < 3 else rbc[1]
                pc = pcs[cc]
                op("dve", lambda e, cc=cc, pc=pc, r_=r_: e.scalar_tensor_tensor(out=cqTb[:, cc, :], in0=pc[:], scalar=gq_fm[:, cc:cc + 1], in1=r_[:], op0=ALU.mult, op1=ALU.mult), [pc, gq_fm, r_], [cqTb])
            active = []
            nxt_tile = 0
            while active or nxt_tile < 4:
                if len(active) < 2 and nxt_tile < 4:
                    active.append(ga_tile(blk, nxt_tile, wA))
                    nxt_tile += 1
                for g_ in list(active):
                    try:
                        next(g_)
                    except StopIteration:
                        active.remove(g_)
            fm = fm_gen(blk)

            def fm_step():
                next(fm, None)

            for seg, base in (("dq", 544), ("dk", 1312), ("dv", 2080)):
                w0 = load_w512("w_in", D, base, 512)
                w1 = load_w512("w_in", D, base + 512, 256)
                for i in range(4):
                    t = blk * 4 + i
                    bsd = BSETS[i % 2]
                    pD = PS["W"].get()
                    for (wsl, n0, n1) in ((w0, 0, 512), (w1, 512, 768)):
                        for kc in range(8):
                            op("pe", lambda e, kc=kc, i=i, pD=pD, wsl=wsl, n0=n0, n1=n1: e.matmul(pD[:, n0:n1], lhsT=hT[:, kc, i * 128:(i + 1) * 128], rhs=wsl[:, kc, 0:n1 - n0], start=(kc == 0), stop=(kc == 7)), [hT, wsl], [pD])
                    pDv = pD[:, 0:768].rearrange("p (h c) -> p h c", h=12)
                    if seg == "dv":
                        op("act", lambda e, pD=pD, i=i: e.activation(out=dvb[:], in_=pD[:, 0:768], func=AF.Copy), [pD], [dvb])
                        dma("pool", l_st["dv"], dvs[tok0 + i * 128:tok0 + (i + 1) * 128, :, :].rearrange("p h c -> p (h c)"), dvb[:], reads=[dvb], writes=[R_dvs])
                        fm_step()
                        continue
                    dtok, dTb = (dq_tok, dqTb) if seg == "dq" else (dk_tok, dkTb)
                    op("act", lambda e, pDv=pDv, dtok=dtok: e.activation(out=dtok[:, :, 16:64], in_=pDv[:, :, 16:64], func=AF.Copy), [pD], [dtok])
                    rope_apply(lambda lo, hi, pDv=pDv: pDv[:, :, lo:hi], [pD], 12, 8, t, lambda lo, hi, dtok=dtok: dtok[:, :, lo:hi], dtok, bsd)
                    fm_step()
                    fm_step()
                    pt = PS["T"].get()
                    for j in range(6):
                        op("pe", lambda e, j=j, dtok=dtok, pt=pt: e.transpose(out=pt[:, j, :], in_=dtok[:, 2 * j:2 * j + 2, :].rearrange("p h c -> p (h c)"), identity=identb[:]), [dtok, identb], [pt])
                    op("dve", lambda e, pt=pt, dTb=dTb, i=i: e.tensor_copy(out=dTb[:, :, i * 128:(i + 1) * 128], in_=pt[:, 0:6, :]), [pt], [dTb])
                    fm_step()
            dma("pool", l_st["dq"], dqTs[:, :, tok0:tok0 + 512].rearrange("j p t -> p j t"), dqTb[:], reads=[dqTb], writes=[R_dqTs])
            dma("pool", l_st["dk"], dkTs[:, :, tok0:tok0 + 512].rearrange("j p t -> p j t"), dkTb[:], reads=[dkTb], writes=[R_dkTs])
            Sx.mark("dil done")
            for _ in fm:
                pass
        Sx.barrier()

    patt = contextlib.ExitStack()

    def sbt(name, shape, dt):
        return Buf(patt.enter_context(nc.sbuf_tensor(name, shape, dt)), name)

    pT_pool = Pool([sbt("pTs%d" % i, [128, 512], BF16) for i in range(3)])
    pTm_pool = Pool([sbt("pTm%d" % i, [128, 512], BF16) for i in range(2)])
    lrow = sbt("lrow", [128, 512], F32)
    rl = sbt("rl", [64, 512], F32)
    o_pool = Pool([sbt("o_st%d" % i, [64, 512], BF16) for i in range(2)])
    l_o = [Sx.lane("l_o%d" % i) for i in range(2)]

    lrow2 = Pool([lrow, sbt("lrow_b", [128, 512], F32)])
    bg_holder = {}
    bg_list = cast_pieces(BG_W)
    bg_pos = [0]

    def bg_emit(k):
        for _ in range(k):
            if bg_pos[0] < len(bg_list):
                bg_holder["c"].emit(bg_list[bg_pos[0]])
                bg_pos[0] += 1

    def finish_a(pO):
        lr = lrow2.get()
        op("act", lambda e: e.activation(out=lr[64:65, :], in_=pO[64:65, :], func=AF.Copy), [pO], [lr])
        return lr

    def finish_b(pO, lr, pB, dst_ap, dst_res):
        op("pe", lambda e: e.matmul(pB[0:64, :], lhsT=onesf[64:65, 0:64], rhs=lr[64:65, :], start=True, stop=True), [onesf, lr], [pB])
        op("dve", lambda e: e.reciprocal(out=rl[:], in_=pB[0:64, :]), [pB], [rl])
        oi = o_pool.i % 2
        ost = o_pool.get()
        op("dve", lambda e: e.tensor_tensor(out=ost[:], in0=rl[:], in1=pO[0:64, :], op=ALU.mult), [pO, rl], [ost])
        dma("pool", l_o[oi], dst_ap, ost[:], reads=[ost], writes=[dst_res])

    def run_stream(steps, look, defer):
        n = len(steps)
        pend = []
        for idx in range(n + look):
            if idx < n:
                st = steps[idx]
                for f_ in st.get("pre", ()):
                    f_()
                st["S"]()
            j = idx - look
            if j >= 0:
                steps[j]["PV"]()
                if steps[j].get("fin_a"):
                    pend.append([defer, steps[j]["fin_a"](), steps[j]["fin_b"]])
            for p in pend:
                p[0] -= 1
            while pend and pend[0][0] < 0:
                _, tok, fb = pend.pop(0)
                fb(tok)
        for _, tok, fb in pend:
            fb(tok)

    with contextlib.ExitStack() as pb:
        def sbb(name, shape, dt):
            return Buf(pb.enter_context(nc.sbuf_tensor(name, shape, dt)), name)

        bg_holder["c"] = Caster(pb, "g", 4, ["dve", "pool"])
        pS2 = Pool([psum(pb, "pS2b%d" % i, [128, 2, 512]) for i in range(2)])
        pOp = Pool([psum(pb, "pOb%d" % i, [128, 512]) for i in range(2)])
        pBp = Pool([psum(pb, "pBb%d" % i, [128, 512]) for i in range(1)])
        pT2 = Pool([sbb("pT2_%d" % i, [128, 2, 512], BF16) for i in range(3)])
        kTh = [sbb("kTh%d" % i, [128, S], BF16) for i in range(2)]
        qTh = [sbb("qTh%d" % i, [128, S], BF16) for i in range(2)]
        vh = [sbb("vh%d" % i, [128, NT, 65], BF16) for i in range(2)]
        l_b = [[Sx.lane("l_b%d_%d" % (i, j)) for j in range(3)] for i in range(2)]
        for i in range(2):
            op("pool", lambda e, i=i: e.memset(vh[i][:, :, 64:65], 1.0), [], [vh[i]])
        sc_mla = 96 ** -0.5

        def load_head(h):
            s = h % 2
            dma("sp", l_b[s][0], kTh[s][0:96, :], kTs[h], reads=[R_kTs], writes=[kTh[s]])
            dma("sp", l_b[s][1], qTh[s][0:96, :], qTs[h], reads=[R_qTs], writes=[qTh[s]])
            dma("sp", l_b[s][2], vh[s][:, :, 0:64], vs[:, h, :].rearrange("(t p) c -> p t c", p=128), reads=[R_vs], writes=[vh[s]])

        load_head(0)
        steps = []
        NP = NT // 2
        for h in range(8):
            s = h % 2
            for qb in range(NB):
                state = {}
                for p in range(NP):
                    st = {}

                    def S_(h=h, s=s, qb=qb, p=p, st=st, state=state):
                        if p == 0:
                            state["pO"] = pOp.get()
                        ps2 = pS2.get()
                        for j in range(2):
                            kt = 2 * p + j
                            op("pe", lambda e, kt=kt, j=j, ps2=ps2: e.matmul(ps2[:, j, :], lhsT=kTh[s][0:96, kt * 128:(kt + 1) * 128], rhs=qTh[s][0:96, qb * 512:(qb + 1) * 512], start=True, stop=True), [kTh[s], qTh[s]], [ps2])
                        pt = pT2.get()
                        op("act", lambda e, ps2=ps2, pt=pt: e.activation(out=pt[:].rearrange("p a b -> p (a b)"), in_=ps2[:].rearrange("p a b -> p (a b)"), func=AF.Exp, scale=sc_mla), [ps2], [pt])
                        st["pt"] = pt

                    def PV_(h=h, s=s, qb=qb, p=p, st=st, state=state):
                        pO = state["pO"]
                        pt = st["pt"]
                        for j in range(2):
                            kt = 2 * p + j
                            op("pe", lambda e, kt=kt, j=j, pt=pt, pO=pO: e.matmul(pO[0:65, :], lhsT=vh[s][:, kt, 0:65], rhs=pt[:, j, :], start=(kt == 0), stop=(kt == NT - 1)), [vh[s], pt], [pO])

                    st["S"] = S_
                    st["PV"] = PV_
                    st["pre"] = []
                    if p == 0 and qb == 1 and h + 1 < 8:
                        st["pre"].append(lambda h=h: load_head(h + 1))
                    if p == 0:
                        st["pre"].append(lambda: bg_emit(2))
                    if p == NP - 1:
                        st["fin_a"] = (lambda state=state: (state["pO"], finish_a(state["pO"])))
                        st["fin_b"] = (lambda tok, h=h, qb=qb: finish_b(tok[0], tok[1], pBp.get(), omlaTs[h, :, qb * 512:(qb + 1) * 512], R_omla))
                    steps.append(st)
        run_stream(steps, look=1, defer=2)
        bg_emit(len(bg_list))
        Sx.barrier()

    with contextlib.ExitStack() as pc:
        def sbc(name, shape, dt):
            return Buf(pc.enter_context(nc.sbuf_tensor(name, shape, dt)), name)

        L2 = S // 16
        KP = min(128, L2)
        NK2 = L2 // KP
        pSc = Pool([psum(pc, "pSc%d" % i, [128, 512]) for i in range(5)])
        pOc = Pool([psum(pc, "pOc%d" % i, [128, 512]) for i in range(2)])
        pBc = Pool([psum(pc, "pBc%d" % i, [128, 512]) for i in range(1)])
        pTc = Pool([sbc("pTc%d" % i, [128, 512], BF16) for i in range(6)])
        pMc = Pool([sbc("pMc%d" % i, [128, 512], BF16) for i in range(6)])
        NM01 = len([1 for (g_, _r) in mlist if g_ < 2])
        masks = sbc("masks_sb", [128, NM01, 512], BF16)
        masks2 = sbc("masks2_sb", [128, 2, 512], BF16)
        l_m = Sx.lane("l_masks")
        dma("sp", l_m, masks2[:], masks2_d.rearrange("m p q -> p m q"), writes=[masks2])
        dma("sp", l_m, masks[:], masks_d[0:NM01].rearrange("m p q -> p m q"), writes=[masks])
        mbias = sbc("mbias_sb", [128, NM01, 512], BF16)
        dma("sp", l_m, mbias[:], mbias_d[0:NM01].rearrange("m p q -> p m q"), writes=[mbias])
        dkT = [[sbc("dkT%d_%d" % (b, g), [128, S], BF16) for g in range(2)] for b in range(2)]
        dqT = [[sbc("dqT%d_%d" % (b, g), [128, S], BF16) for g in range(2)] for b in range(2)]
        dvh = [[sbc("dvh%d_%d" % (b, g), [128, NT, 65], BF16) for g in range(2)] for b in range(2)]
        dk2n = sbc("dk2n", [128, S], BF16)
        dq2n = sbc("dq2n", [128, S], BF16)
        dk2c = sbc("dk2c", [128, 16, L2], BF16)
        dq2c = sbc("dq2c", [128, 16, L2], BF16)
        v2c = sbc("v2c", [128, 16 * NK2, 65], BF16)
        acc2n = sbc("acc2n", [128, S], F32)
        sums = [sbc("sums%d" % i, [128, 512], F32) for i in range(2)]
        l_cd = [Sx.lane("l_cd%d" % i) for i in range(3)]
        l_c2 = [Sx.lane("l_c2_%d" % i) for i in range(3)]
        print("SBUF remaining in phase C", nc.sbuf_bytes_remaining)
        for b in range(2):
            for g in range(2):
                op("pool", lambda e, b=b, g=g: e.memset(dvh[b][g][:, :, 64:65], 1.0), [], [dvh[b][g]])
                z0 = 64 * (1 - b)
                op("pool", lambda e, b=b, g=g, z0=z0: e.memset(dqT[b][g][z0:z0 + 64, :], 0.0), [], [dqT[b][g]])
        op("pool", lambda e: e.memset(v2c[:, :, 64:65], 1.0), [], [v2c])
        sc_dil = 64 ** -0.5
        midx = {gr: i for i, gr in enumerate(mlist)}

        def load_j(j):
            b = j % 2
            for g in range(2):
                hd = g * 4 + j
                pair = hd // 2
                dma("sp", l_cd[0], dkT[b][g][:], dkTs[pair], reads=[R_dkTs], writes=[dkT[b][g]])
                rq = 64 * (hd % 2)
                dma("sp", l_cd[1], dqT[b][g][rq:rq + 64, :], dqTs[pair, rq:rq + 64, :], reads=[R_dqTs], writes=[dqT[b][g]])
                dma("sp", l_cd[2], dvh[b][g][:, :, 0:64], dvs[:, hd, :].rearrange("(t p) c -> p t c", p=128), reads=[R_dvs], writes=[dvh[b][g]])

        def load_qk2(j):
            hd = 8 + j
            pair = hd // 2
            rq = 64 * (hd % 2)
            zq = 64 - rq
            dma("sp", l_c2[0], dk2n[:], dkTs[pair], reads=[R_dkTs], writes=[dk2n])
            op("pool", lambda e: e.memset(dq2n[zq:zq + 64, :], 0.0), [], [dq2n])
            dma("sp", l_c2[1], dq2n[rq:rq + 64, :], dqTs[pair, rq:rq + 64, :], reads=[R_dqTs], writes=[dq2n])
            op("act", lambda e: e.activation(out=dk2c[:], in_=dk2n[:].rearrange("p (m r) -> p r m", r=16), func=AF.Copy), [dk2n], [dk2c])
            op("act", lambda e: e.activation(out=dq2c[:], in_=dq2n[:].rearrange("p (m r) -> p r m", r=16), func=AF.Copy), [dq2n], [dq2c])

        def load_v2(j):
            hd = 8 + j
            for r in range(16):
                pass
            dma("sp", l_c2[2], v2c[0:KP, :, 0:64].rearrange("p (r k) c -> p r k c", r=16),
                dvs[:, hd, :].rearrange("(k p r) c -> p r k c", p=KP, r=16), reads=[R_dvs], writes=[v2c])

        load_j(0)
        load_qk2(0)
        load_v2(0)
        steps = []
        for j in range(4):
            b = j % 2
            for r in range(16):
                state = {}
                for k2 in range(NK2):
                    st = {"pre": []}

                    def S_(r=r, k2=k2, st=st, state=state):
                        if k2 == 0:
                            state["pO"] = pOc.get()
                        pS = pSc.get()
                        op("pe", lambda e, pS=pS: e.matmul(pS[0:KP, 0:L2], lhsT=dk2c[:, r, k2 * KP:(k2 + 1) * KP], rhs=dq2c[:, r, :], start=True, stop=True), [dk2c, dq2c], [pS])
                        pt = pTc.get()
                        op("act", lambda e, pS=pS, pt=pt: e.activation(out=pt[0:KP, 0:L2], in_=pS[0:KP, 0:L2], func=AF.Exp, scale=sc_dil), [pS], [pt])
                        pm = pMc.get()
                        op("dve", lambda e, pt=pt, pm=pm: e.tensor_tensor(out=pm[0:KP, 0:L2], in0=pt[0:KP, 0:L2], in1=masks2[0:KP, k2, 0:L2], op=ALU.mult), [pt, masks2], [pm])
                        st["pm"] = pm

                    def PV_(r=r, k2=k2, st=st, state=state):
                        pO = state["pO"]
                        pm = st["pm"]
                        op("pe", lambda e, pm=pm, pO=pO: e.matmul(pO[0:65, 0:L2], lhsT=v2c[0:KP, r * NK2 + k2, 0:65], rhs=pm[0:KP, 0:L2], start=(k2 == 0), stop=(k2 == NK2 - 1)), [v2c, pm], [pO])

                    st["S"] = S_
                    st["PV"] = PV_
                    if k2 == NK2 - 1:
                        def fa(r=r, state=state):
                            pO = state["pO"]
                            op("act", lambda e: e.activation(out=acc2n[0:65, :].rearrange("p (m r) -> p r m", r=16)[:, r, :], in_=pO[0:65, 0:L2], func=AF.Copy), [pO], [acc2n])
                            return None
                        st["fin_a"] = fa
                        st["fin_b"] = (lambda tok: None)
                    steps.append(st)
            for qb in range(NB):
                q0 = qb * 512
                work = []
                for g in range(2):
                    for kt in range(NT):
                        rel = kt * 128 - q0
                        if (g, rel) in midx:
                            work.append((g, kt, midx[(g, rel)]))
                state = {}
                nw = len(work)
                for wi, (g, kt, mi) in enumerate(work):
                    st = {"pre": []}

                    def S_(b=b, g=g, kt=kt, mi=mi, q0=q0, wi=wi, st=st, state=state):
                        if wi == 0:
                            state["pO"] = pOc.get()
                        pS = pSc.get()
                        pe_mask = (wi % 2 == 1)
                        op("pe", lambda e, pS=pS: e.matmul(pS[:], lhsT=dkT[b][g][:, kt * 128:(kt + 1) * 128], rhs=dqT[b][g][:, q0:q0 + 512], start=True, stop=not pe_mask), [dkT[b][g], dqT[b][g]], [pS])
                        if pe_mask:
                            op("pe", lambda e, pS=pS: e.matmul(pS[:], lhsT=identb[:], rhs=mbias[:, mi, :], start=False, stop=True), [identb, mbias], [pS])
                        pt = pTc.get()
                        op("act", lambda e, pS=pS, pt=pt: e.activation(out=pt[:], in_=pS[:], func=AF.Exp, scale=sc_dil), [pS], [pt])
                        if pe_mask:
                            st["pm"] = pt
                        else:
                            pm = pMc.get()
                            op("dve", lambda e, pt=pt, pm=pm: e.tensor_tensor(out=pm[:], in0=pt[:], in1=masks[:, mi, :], op=ALU.mult), [pt, masks], [pm])
                            st["pm"] = pm

                    def PV_(b=b, g=g, kt=kt, wi=wi, nw=nw, st=st, state=state):
                        pO = state["pO"]
                        pm = st["pm"]
                        op("pe", lambda e, pm=pm, pO=pO: e.matmul(pO[0:65, :], lhsT=dvh[b][g][:, kt, 0:65], rhs=pm[:], start=(wi == 0), stop=(wi == nw - 1)), [dvh[b][g], pm], [pO])

                    st["S"] = S_
                    st["PV"] = PV_
                    if wi == 0 and qb == 0 and j + 1 < 4:
                        st["pre"].append(lambda j=j: load_qk2(j + 1))
                    if wi == 0 and qb == 1 and j + 1 < 4:
                        st["pre"].append(lambda j=j: load_j(j + 1))
                        st["pre"].append(lambda j=j: load_v2(j + 1))
                    if wi == nw - 1:
                        def fa(state=state, q0=q0, qb=qb, j=j):
                            pO = state["pO"]
                            sm = sums[(j * NB + qb) % 2]
                            op("dve", lambda e: e.tensor_tensor(out=sm[0:65, :], in0=acc2n[0:65, q0:q0 + 512], in1=pO[0:65, :], op=ALU.add), [acc2n, pO], [sm])
                            return sm

                        def fb(sm, j=j, q0=q0):
                            pB = pBc.get()
                            op("pe", lambda e: e.matmul(pB[0:64, :], lhsT=onesf[64:65, 0:64], rhs=sm[64:65, :], start=True, stop=True), [onesf, sm], [pB])
                            op("dve", lambda e: e.reciprocal(out=rl[:], in_=pB[0:64, :]), [pB], [rl])
                            oi = o_pool.i % 2
                            ost = o_pool.get()
                            op("dve", lambda e: e.tensor_tensor(out=ost[:], in0=rl[:], in1=sm[0:64, :], op=ALU.mult), [sm, rl], [ost])
                            dma("pool", l_o[oi], odilTs[j, :, q0:q0 + 512], ost[:], reads=[ost], writes=[R_odil])
                        st["fin_a"] = fa
                        st["fin_b"] = fb
                    steps.append(st)
        run_stream(steps, look=4, defer=5)
        Sx.barrier()

    with contextlib.ExitStack() as pd_:
        def sbd(name, shape, dt):
            return Buf(pd_.enter_context(nc.sbuf_tensor(name, shape, dt)), name)

        set_psum(pd_, "d", A=2, O=2, T=2, F=2)
        load_gains(pd_, ["mem_norm"])
        memx = sbd("memx", [128, 2, 1024], F32)
        memhT = sbd("memhT", [128, 8, 256], BF16)
        wkv = sbd("wkv", [128, 8, 1024], BF16)
        kmT = sbd("kmT", [128, 4, 256], BF16)
        vm = sbd("vm", [128, 2, 512], BF16)
        mqT = [sbd("mqT%d" % i, [128, S], BF16) for i in range(2)]
        rlm = sbd("rlm", [128, 512], F32)
        om_pool = Pool([sbd("om_st%d" % i, [128, 512], BF16) for i in range(2)])
        l_d = Sx.lane("l_d")
        l_dq = [Sx.lane("l_dq%d" % i) for i in range(2)]
        l_om = [Sx.lane("l_om%d" % i) for i in range(2)]
        dma("sp", l_d, memx[:], mem_d.rearrange("(i p) d -> p i d", p=128), writes=[memx])
        dma("sp", l_d, wkv[:], wb["mem_w_kv"].rearrange("(kc p) n -> p kc n", p=128), reads=[wb_r["mem_w_kv"]], writes=[wkv])
        for i in range(2):
            rmsnorm_T(memx[:, i, :], [memx], D, gbc["mem_norm"], memhT, lambda nk, i=i: memhT[:, 0:nk, i * 128:(i + 1) * 128])
        for h in range(4):
            pk = PS["A"].get()
            for kc in range(8):
                op("pe", lambda e, kc=kc, h=h, pk=pk: e.matmul(pk[:, 0:256], lhsT=wkv[:, kc, h * 128:(h + 1) * 128], rhs=memhT[:, kc, :], start=(kc == 0), stop=(kc == 7)), [wkv, memhT], [pk])
            op("dve", lambda e, h=h, pk=pk: e.tensor_copy(out=kmT[:, h, :], in_=pk[:, 0:256]), [pk], [kmT])
        for i in range(2):
            pv = PS["A"].get()
            for kc in range(8):
                op("pe", lambda e, kc=kc, i=i, pv=pv: e.matmul(pv[:], lhsT=memhT[:, kc, i * 128:(i + 1) * 128], rhs=wkv[:, kc, 512:1024], start=(kc == 0), stop=(kc == 7)), [memhT, wkv], [pv])
            op("dve", lambda e, i=i, pv=pv: e.tensor_copy(out=vm[:, i, :], in_=pv[:]), [pv], [vm])
        sc_mem = 128 ** -0.5
        for h in range(4):
            s = h % 2
            dma("sp", l_dq[s], mqT[s][:], mqTs[h], reads=[R_mqTs], writes=[mqT[s]])
            for qb in range(NB):
                pO = PS["O"].get()
                pL = PS["F"].get()
                pts = []
                for mt in range(2):
                    pS = PS["A"].get()
                    op("pe", lambda e, mt=mt, h=h, s=s, qb=qb, pS=pS: e.matmul(pS[:], lhsT=kmT[:, h, mt * 128:(mt + 1) * 128], rhs=mqT[s][:, qb * 512:(qb + 1) * 512], start=True, stop=True), [kmT, mqT[s]], [pS])
                    pt = pT_pool.get()
                    op("act", lambda e, pS=pS, pt=pt: e.activation(out=pt[:], in_=pS[:], func=AF.Exp, scale=sc_mem), [pS], [pt])
                    pts.append(pt)
                for mt in range(2):
                    op("pe", lambda e, mt=mt, h=h, pO=pO, pts=pts: e.matmul(pO[:], lhsT=vm[:, mt, h * 128:(h + 1) * 128], rhs=pts[mt][:], start=(mt == 0), stop=(mt == 1)), [vm, pts[mt]], [pO])
                for mt in range(2):
                    op("pe", lambda e, mt=mt, pL=pL, pts=pts: e.matmul(pL[:, 0:512], lhsT=onesb[:], rhs=pts[mt][:], start=(mt == 0), stop=(mt == 1)), [onesb, pts[mt]], [pL])
                op("dve", lambda e, pL=pL: e.reciprocal(out=rlm[:], in_=pL[:, 0:512]), [pL], [rlm])
                oi = om_pool.i % 2
                ost = om_pool.get()
                op("dve", lambda e, pO=pO, ost=ost: e.tensor_tensor(out=ost[:], in0=rlm[:], in1=pO[:], op=ALU.mult), [pO, rlm], [ost])
                dma("pool", l_om[oi], omemTs[h, :, qb * 512:(qb + 1) * 512], ost[:], reads=[ost], writes=[R_omem])
        Sx.barrier()

    patt.close()
    with contextlib.ExitStack() as pe_:
        def sbe(name, shape, dt):
            return Buf(pe_.enter_context(nc.sbuf_tensor(name, shape, dt)), name)

        print("SBUF remaining at phase E start", nc.sbuf_bytes_remaining)
        set_psum(pe_, "e", A=6, T=2)
        load_gains(pe_, ["ffn2_norm", "final_norm"])
        alloc_ffn(pe_, "_e", 3, 256)
        xbE, hTE = FB["xb"], FB["hT"]
        mixT = hTE
        womla = sbe("womla", [128, 4, 1024], BF16)
        wodil = sbe("wodil", [128, 2, 1024], BF16)
        womem = sbe("womem", [128, 4, 1024], BF16)
        wout = sbe("wout", [128, 8, 1024], BF16)
        omla = sbe("omla", [128, 4, 512], BF16)
        odil = sbe("odil", [128, 2, 512], BF16)
        omem = sbe("omem", [128, 4, 512], BF16)
        l_g3 = [Sx.lane("l_g3_%d" % i) for i in range(2)]
        m1 = sbe("m1", [128, 512], F32)
        m2 = sbe("m2", [128, 512], F32)
        l_e = Sx.lane("l_e")
        l_ein = [Sx.lane("l_ein%d" % i) for i in range(3)]
        l_out = Sx.lane("l_out")
        dma("sp", l_e, womla[:], wb["mla_w_o"].rearrange("(h p) n -> p h n", p=128), reads=[wb_r["mla_w_o"]], writes=[womla])
        dma("sp", l_e, wodil[:], wb["dil_w_o"].rearrange("(h p) n -> p h n", p=128), reads=[wb_r["dil_w_o"]], writes=[wodil])
        dma("sp", l_e, womem[:], wb["mem_w_o"].rearrange("(h p) n -> p h n", p=128), reads=[wb_r["mem_w_o"]], writes=[womem])
        dma("sp", l_e, wout[:], wb["w_out"].rearrange("(h p) n -> p h n", p=128), reads=[wb_r["w_out"]], writes=[wout])
        load_wd("ffn2")
        print("SBUF remaining after phase E alloc", nc.sbuf_bytes_remaining)
        xbs = [xbE, sbe("xb2_e", [128, 4, 1024], F32)]
        l_x2 = [l_x, Sx.lane("l_x_b")]
        l_out2 = [l_out, Sx.lane("l_out_b")]
        g3s = [sbe("g3b_%d" % i, [128, 3, 512], BF16) for i in range(2)]
        m1s = [m1, sbe("m1b", [128, 512], F32)]
        m2s = [m2, sbe("m2b", [128, 512], F32)]
        m3s = [sbe("m3a", [128, 512], F32), sbe("m3b", [128, 512], F32)]
        omla_v = omlaTs.rearrange("h p t -> (h p) t").rearrange("(hp q) t -> q hp t", q=128)
        odil_v = odilTs.rearrange("h p t -> (h p) t").rearrange("(hp q) t -> q hp t", q=128)
        gts_v = gTs.rearrange("(b c) p t -> c p b t", b=3)

        def issue_loads(blk):
            t0 = blk * 512
            dma("sp", l_ein[0], omla[:], omla_v[:, :, t0:t0 + 512], reads=[R_omla], writes=[omla])
            dma("sp", l_ein[1], odil[:], odil_v[:, :, t0:t0 + 512], reads=[R_odil], writes=[odil])
            dma("sp", l_ein[2], omem[:], omemTs[:, :, t0:t0 + 512].rearrange("h p t -> p h t"), reads=[R_omem], writes=[omem])
            xb_ = xbs[blk % 2]
            dma("sp", l_x2[blk % 2], xb_[:], x1s[t0:t0 + 512, :].rearrange("(i p) d -> p i d", p=128), reads=[R_x1s], writes=[xb_])

        def load_g3(blk, c):
            k = (blk * 8 + c) % 2
            dma("sp", l_g3[k], g3s[k][:], gts_v[c][:, :, blk * 512:(blk + 1) * 512], reads=[R_gTs], writes=[g3s[k]])
            return g3s[k]

        issue_loads(0)
        nxt = load_g3(0, 0)
        for blk in range(NB):
            tok0 = blk * 512
            xb_ = xbs[blk % 2]
            for c in range(8):
                cs = slice(c * 128, (c + 1) * 128)
                gT = nxt
                if c < 7:
                    nxt = load_g3(blk, c + 1)
                elif blk + 1 < NB:
                    nxt = load_g3(blk + 1, 0)
                k = (blk * 8 + c) % 2
                m1_, m2_, m3_ = m1s[k], m2s[k], m3s[k]
                p1 = PS["A"].get()
                for h in range(4):
                    op("pe", lambda e, h=h, cs=cs, p1=p1: e.matmul(p1[:], lhsT=womla[:, h, cs], rhs=omla[:, h, :], start=(h == 0), stop=(h == 3)), [womla, omla], [p1])
                p2 = PS["A"].get()
                for h in range(2):
                    op("pe", lambda e, h=h, cs=cs, p2=p2: e.matmul(p2[:], lhsT=wodil[:, h, cs], rhs=odil[:, h, :], start=(h == 0), stop=(h == 1)), [wodil, odil], [p2])
                p3 = PS["A"].get()
                for h in range(4):
                    op("pe", lambda e, h=h, cs=cs, p3=p3: e.matmul(p3[:], lhsT=womem[:, h, cs], rhs=omem[:, h, :], start=(h == 0), stop=(h == 3)), [womem, omem], [p3])
                op("dve", lambda e, p1=p1, gT=gT, m1_=m1_: e.tensor_tensor(out=m1_[:], in0=gT[:, 0, :], in1=p1[:], op=ALU.mult), [p1, gT], [m1_])
                op("dve", lambda e, p2=p2, gT=gT, m2_=m2_: e.tensor_tensor(out=m2_[:], in0=gT[:, 1, :], in1=p2[:], op=ALU.mult), [p2, gT], [m2_])
                op("dve", lambda e, p3=p3, gT=gT, m3_=m3_: e.tensor_tensor(out=m3_[:], in0=gT[:, 2, :], in1=p3[:], op=ALU.mult), [p3, gT], [m3_])
                op("pool", lambda e, m1_=m1_, m2_=m2_: e.tensor_tensor(out=m1_[:], in0=m1_[:], in1=m2_[:], op=ALU.add), [m1_, m2_], [m1_])
                op("pool", lambda e, c=c, m1_=m1_, m3_=m3_: e.tensor_tensor(out=mixT[:, c, :], in0=m1_[:], in1=m3_[:], op=ALU.add), [m1_, m3_], [mixT])
            if blk + 1 < NB:
                issue_loads(blk + 1)
            for half in range(2):
                for i in range(4):
                    po = PS["A"].get()
                    for c in range(8):
                        op("pe", lambda e, c=c, i=i, half=half, po=po: e.matmul(po[:], lhsT=mixT[:, c, i * 128:(i + 1) * 128], rhs=wout[:, c, half * 512:(half + 1) * 512], start=(c == 0), stop=(c == 7)), [mixT, wout], [po])
                    op("dve", lambda e, i=i, half=half, po=po, xb_=xb_: e.tensor_tensor(out=xb_[:, i, half * 512:(half + 1) * 512], in0=xb_[:, i, half * 512:(half + 1) * 512], in1=po[:], op=ALU.add), [po, xb_], [xb_])
            ffn("ffn2", gbc["ffn2_norm"], xb_)
            gfin = gbc["final_norm"]
            for i in range(4):
                ss = rstd_of(xb_[:, i, :], [xb_], D)
                op("dve", lambda e, i=i, ss=ss, xb_=xb_: e.scalar_tensor_tensor(out=xb_[:, i, :], in0=xb_[:, i, :], scalar=ss[:, 0:1], in1=gfin[:], op0=ALU.mult, op1=ALU.mult), [xb_, ss, gfin], [xb_])
            dma("pool", l_out2[blk % 2], out_d[tok0:tok0 + 512, :].rearrange("(i p) d -> p i d", p=128), xb_[:], reads=[xb_], writes=[R_out])
        Sx.barrier()

    Sx.emit()
    stack.close()
    return nc


_CACHE = {}


def host_consts():
    if "c" not in _CACHE:
        _, masks = make_masks()
        half = 16
        invf = (500000.0 ** (-2.0 * np.arange(half, dtype=np.float32) / 32.0)).astype(np.float32)
        _CACHE["c"] = {
            "ident": np.eye(128, dtype=np.float32),
            "invf": np.ascontiguousarray(np.broadcast_to(invf[None, :], (128, 16))).astype(np.float32),
            "masks": masks,
            "mbias": ((masks.astype(np.float32) - 1.0) * 30000.0).astype(ml_dtypes.bfloat16),
        }
    return _CACHE["c"]


def make_masks2(S):
    L2 = S // 16
    KP = min(128, L2)
    m = np.zeros((2, 128, 512), np.float32)
    kk = np.arange(128)[:, None]
    qq = np.arange(512)[None, :]
    for k2 in range(2):
        m[k2] = (np.abs(k2 * KP + kk - qq) <= 64).astype(np.float32)
    return m.astype(ml_dtypes.bfloat16)


def make_in_map(b, S, inputs):
    c = host_consts()
    m = {
        "x": np.ascontiguousarray(inputs["x"][b]),
        "mem": np.ascontiguousarray(inputs["mem"][b]),
        "pos": np.ascontiguousarray(np.asarray(inputs["positions"][b]).astype(np.int32).reshape(S // 128, 128).T),
        "ident": c["ident"], "invf": c["invf"], "masks": c["masks"], "masks2": make_masks2(S), "mbias": c["mbias"],
    }
    for n, K, M in WEIGHTS:
        m[n] = np.ascontiguousarray(np.asarray(inputs[n], dtype=np.float32).reshape(K, M))
    for n, M in GAINS:
        m[n] = np.ascontiguousarray(np.asarray(inputs[n], dtype=np.float32).reshape(1, M))
    return m


def kernel(**inputs):
    inputs = {k: np.asarray(v) for k, v in inputs.items()}
    B, S, _ = inputs["x"].shape
    nc = build(S)
    in_maps = [make_in_map(b, S, inputs) for b in range(B)]
    res = run_bass_kernel_spmd(nc, in_maps, core_ids=list(range(B)))
    return np.stack([np.asarray(r["out"]) for r in res.results], axis=0).astype(np.float32)
```

```python
import contextlib
import math
import os
import numpy as np
import ml_dtypes
import concourse.bass as bass
import concourse.mybir as mybir
from concourse.bass_utils import run_bass_kernel_spmd

F32 = mybir.dt.float32
BF16 = mybir.dt.bfloat16
I32 = mybir.dt.int32
ALU = mybir.AluOpType
AF = mybir.ActivationFunctionType

D = 1024
DFF = 2816
NMEM = 256
DIN = 6432
EPS = 1e-6
DIL_GROUPS = ((128, 1), (512, 4), (2048, 16))


class Res:
    __slots__ = ("name", "w", "rs")

    def __init__(self, name):
        self.name = name
        self.w = None
        self.rs = []


class Lane:
    def __init__(self, S, name):
        self.sem = S.newsem(name)
        self.n = 0


class Buf:
    def __init__(self, t, name):
        self.t = t
        self.r = Res(name)

    def __getitem__(self, k):
        return self.t[k]


class Sched:
    ENGS = ("pe", "dve", "act", "pool", "sp")

    def __init__(self, nc, stack):
        self.nc = nc
        self.stack = stack
        self.sems = {}
        self.cnt = {}
        self.seen = {k: {} for k in self.ENGS}
        self.prog = {k: [] for k in self.ENGS}
        self.lanes = []
        for k in self.ENGS:
            self.sems[k] = self.newsem("s_" + k)
            self.cnt[k] = 0

    def newsem(self, name):
        return self.stack.enter_context(self.nc.semaphore(name))

    def lane(self, name):
        l = Lane(self, name)
        self.lanes.append(l)
        return l

    def _deps(self, eng, reads, writes):
        deps = {}

        def add(tok, raw):
            if tok is None:
                return
            sem, val, src = tok
            if src == eng:
                if eng == "pe":
                    return
            key = id(sem)
            if key not in deps or deps[key][1] < val:
                deps[key] = (sem, val)

        for r in reads:
            add(r.w, True)
        for w in writes:
            add(w.w, False)
            for t in w.rs:
                add(t, False)
        for key, (sem, val) in deps.items():
            if self.seen[eng].get(key, 0) < val:
                self.prog[eng].append(("w", sem, val))
                self.seen[eng][key] = val

    def _commit(self, tok, reads, writes):
        for r in reads:
            r.rs.append(tok)
            if len(r.rs) > 64:
                best = {}
                for t in r.rs:
                    k = id(t[0])
                    if k not in best or best[k][1] < t[1]:
                        best[k] = t
                r.rs = list(best.values())
        for w in writes:
            w.w = tok
            w.rs = []

    def _skip(self):
        self.total = getattr(self, "total", 0) + 1
        return self.total > int(os.environ.get("KSTOP", "1000000000"))

    def op(self, eng, fn, reads=(), writes=()):
        if self._skip():
            return
        reads = [b.r if isinstance(b, Buf) else b for b in reads]
        writes = [b.r if isinstance(b, Buf) else b for b in writes]
        self._deps(eng, reads, writes)
        self.cnt[eng] += 1
        self.prog[eng].append(("i", fn, self.sems[eng], 1))
        self._commit((self.sems[eng], self.cnt[eng], eng), reads, writes)

    def dma(self, eng, lane, out, in_, reads=(), writes=()):
        if self._skip():
            return
        reads = [b.r if isinstance(b, Buf) else b for b in reads]
        writes = [b.r if isinstance(b, Buf) else b for b in writes]
        self._deps(eng, reads, writes)
        if lane.n > 0 and self.seen[eng].get(id(lane.sem), 0) < 16 * lane.n:
            self.prog[eng].append(("w", lane.sem, 16 * lane.n))
            self.seen[eng][id(lane.sem)] = 16 * lane.n
        lane.n += 1
        self.prog[eng].append(("i", lambda e: e.dma_start(out=out, in_=in_), lane.sem, 16))
        self._commit((lane.sem, 16 * lane.n, "dma"), reads, writes)

    def mark(self, name):
        if os.environ.get("KDEBUG"):
            print("mark", name, getattr(self, "total", 0))

    def barrier(self):
        if os.environ.get("KDEBUG"):
            print("barrier at op count", getattr(self, "total", 0))
        for eng in self.ENGS:
            for k in self.ENGS:
                if k == eng or self.cnt[k] == 0:
                    continue
                key = id(self.sems[k])
                if self.seen[eng].get(key, 0) < self.cnt[k]:
                    self.prog[eng].append(("w", self.sems[k], self.cnt[k]))
                    self.seen[eng][key] = self.cnt[k]
            for l in self.lanes:
                if l.n == 0:
                    continue
                key = id(l.sem)
                if self.seen[eng].get(key, 0) < 16 * l.n:
                    self.prog[eng].append(("w", l.sem, 16 * l.n))
                    self.seen[eng][key] = 16 * l.n

    def emit(self):
        prog = self.prog

        def run(e, lst):
            for it in lst:
                if it[0] == "w":
                    e.wait_ge(it[1], it[2])
                else:
                    it[1](e).then_inc(it[2], it[3])

        with self.nc.Block() as block:
            @block.tensor
            def _(e):
                run(e, prog["pe"])

            @block.vector
            def _(e):
                run(e, prog["dve"])

            @block.scalar
            def _(e):
                run(e, prog["act"])

            @block.gpsimd
            def _(e):
                run(e, prog["pool"])

            @block.sync
            def _(e):
                run(e, prog["sp"])


def dil_mask_list(S):
    out = []
    for g, (win, d) in enumerate(DIL_GROUPS):
        w = 64 * d
        for rel in range(-4096, 4097, 128):
            if rel - 511 <= w and rel + 127 >= -w:
                out.append((g, rel))
    return out


def make_masks():
    ml = dil_mask_list(4096)
    m = np.zeros((len(ml), 128, 512), np.float32)
    kk = np.arange(128)[:, None]
    qq = np.arange(512)[None, :]
    for i, (g, rel) in enumerate(ml):
        d = DIL_GROUPS[g][1]
        delta = rel + kk - qq
        m[i] = ((np.abs(delta) <= 64 * d) & (delta % d == 0)).astype(np.float32)
    return ml, m.astype(ml_dtypes.bfloat16)


WEIGHTS = [
    ("ffn1_w_gate", D, DFF), ("ffn1_w_up", D, DFF), ("ffn1_w_down", DFF, D),
    ("w_in", D, DIN), ("mla_w_uq", 384, 768), ("mla_w_ukv", 128, 1024),
    ("mem_w_kv", D, 1024), ("mla_w_o", 512, D), ("dil_w_o", 256, D), ("mem_w_o", 512, D),
    ("w_out", D, D), ("ffn2_w_gate", D, DFF), ("ffn2_w_up", D, DFF), ("ffn2_w_down", DFF, D),
]
GAINS = [("ffn1_norm", D), ("mix_norm", D), ("mla_q_norm", 384), ("mla_kv_norm", 128),
         ("mem_norm", D), ("ffn2_norm", D), ("final_norm", D)]


def build(S=4096):
    NT = S // 128
    NB = S // 512
    nc = bass.Bass("TRN2", target_bir_lowering=False)
    stack = contextlib.ExitStack()
    Sx = Sched(nc, stack)
    op, dma = Sx.op, Sx.dma

    x_d = nc.dram_tensor("x", [S, D], F32, kind="ExternalInput").ap()
    mem_d = nc.dram_tensor("mem", [NMEM, D], F32, kind="ExternalInput").ap()
    pos_d = nc.dram_tensor("pos", [128, NT], I32, kind="ExternalInput").ap()
    ident_d = nc.dram_tensor("ident", [128, 128], F32, kind="ExternalInput").ap()
    invf_d = nc.dram_tensor("invf", [128, 16], F32, kind="ExternalInput").ap()
    ml, _ = None, None
    mlist = dil_mask_list(4096)
    NM = len(mlist)
    masks_d = nc.dram_tensor("masks", [NM, 128, 512], BF16, kind="ExternalInput").ap()
    masks2_d = nc.dram_tensor("masks2", [2, 128, 512], BF16, kind="ExternalInput").ap()
    mbias_d = nc.dram_tensor("mbias", [NM, 128, 512], BF16, kind="ExternalInput").ap()
    w_d = {n: nc.dram_tensor(n, [k, m], F32, kind="ExternalInput").ap() for n, k, m in WEIGHTS}
    g_d = {n: nc.dram_tensor(n, [1, m], F32, kind="ExternalInput").ap() for n, m in GAINS}
    out_d = nc.dram_tensor("out", [S, D], F32, kind="ExternalOutput").ap()

    wb = {n: nc.dram_tensor(n + "_bf", [k, m], BF16, kind="Internal").ap() for n, k, m in WEIGHTS}
    wb_r = {n: Res(n + "_bf") for n, _, _ in WEIGHTS}
    x1s = nc.dram_tensor("x1s", [S, D], F32, kind="Internal").ap()
    qTs = nc.dram_tensor("qTs", [8, 96, S], BF16, kind="Internal").ap()
    kTs = nc.dram_tensor("kTs", [8, 96, S], BF16, kind="Internal").ap()
    vs = nc.dram_tensor("vs", [S, 8, 64], BF16, kind="Internal").ap()
    dqTs = nc.dram_tensor("dqTs", [6, 128, S], BF16, kind="Internal").ap()
    dkTs = nc.dram_tensor("dkTs", [6, 128, S], BF16, kind="Internal").ap()
    dvs = nc.dram_tensor("dvs", [S, 12, 64], BF16, kind="Internal").ap()
    mqTs = nc.dram_tensor("mqTs", [4, 128, S], BF16, kind="Internal").ap()
    gTs = nc.dram_tensor("gTs", [24, 128, S], BF16, kind="Internal").ap()
    omlaTs = nc.dram_tensor("omlaTs", [8, 64, S], BF16, kind="Internal").ap()
    odilTs = nc.dram_tensor("odilTs", [4, 64, S], BF16, kind="Internal").ap()
    omemTs = nc.dram_tensor("omemTs", [4, 128, S], BF16, kind="Internal").ap()
    R_x1s, R_qTs, R_kTs, R_vs = Res("x1s"), Res("qTs"), Res("kTs"), Res("vs")
    R_dqTs, R_dkTs, R_dvs, R_mqTs, R_gTs = Res("dqTs"), Res("dkTs"), Res("dvs"), Res("mqTs"), Res("gTs")
    R_omla, R_odil, R_omem, R_out = Res("omla"), Res("odil"), Res("omem"), Res("out")

    def sb(name, shape, dt):
        return Buf(nc.alloc_sbuf_tensor(name, shape, dt), name)

    def ps(name, shape, dt=F32):
        return Buf(nc.alloc_psum_tensor(name, shape, dt), name)

    class Pool:
        def __init__(self, bufs):
            self.bufs = bufs
            self.i = 0

        def get(self):
            b = self.bufs[self.i % len(self.bufs)]
            self.i += 1
            return b

    identf = sb("identf", [128, 128], F32)
    identb = sb("identb", [128, 128], BF16)
    onesb = sb("onesb", [128, 128], BF16)
    onesf = sb("onesf", [128, 64], F32)
    gbc = {}
    GDIM = dict(GAINS)

    def load_gains(ctx, names):
        for n in names:
            gbc[n] = Buf(ctx.enter_context(nc.sbuf_tensor("g_" + n, [128, GDIM[n]], F32)), "g_" + n)
            dma("sp", l_c, gbc[n][:], g_d[n].partition_broadcast(128), writes=[gbc[n]])
    cosT = sb("cosT", [128, NT, 16], F32)
    sinT = sb("sinT", [128, NT, 16], F32)
    cosTd = sb("cosTd", [128, NT, 8], F32)
    sinTd = sb("sinTd", [128, NT, 8], F32)
    l_c = Sx.lane("l_const")
    dma("sp", l_c, identf[:], ident_d, writes=[identf])
    op("dve", lambda e: e.tensor_copy(out=identb[:], in_=identf[:]), [identf], [identb])
    op("dve", lambda e: e.memset(onesb[:], 1.0), [], [onesb])
    op("dve", lambda e: e.memset(onesf[:], 1.0), [], [onesf])

    PS = {}

    def psum(ctx, name, shape, dt=F32):
        return Buf(ctx.enter_context(nc.psum_tensor(name, shape, dt)), name)

    def set_psum(ctx, tag, A=0, O=0, W=0, T=0, F=0):
        PS["F"] = Pool([psum(ctx, "psF%s%d" % (tag, i), [128, 512]) for i in range(F)])
        PS["A"] = Pool([psum(ctx, "psA%s%d" % (tag, i), [128, 512]) for i in range(A)])
        PS["O"] = Pool([psum(ctx, "psO%s%d" % (tag, i), [128, 512]) for i in range(O)])
        PS["W"] = Pool([psum(ctx, "psW%s%d" % (tag, i), [128, 1024]) for i in range(W)])
        PS["T"] = Pool([psum(ctx, "psT%s%d" % (tag, i), [128, 8, 128], BF16) for i in range(T)])

    with nc.sbuf_tensor("posi", [128, NT], I32) as posi_t, nc.sbuf_tensor("posf", [128, NT], F32) as posf_t, \
            nc.sbuf_tensor("invf_sb", [128, 16], F32) as invf_t, nc.sbuf_tensor("ang", [128, NT, 16], F32) as ang_t, \
            nc.sbuf_tensor("kf", [128, NT, 16], F32) as kf_t, nc.sbuf_tensor("ki", [128, NT, 16], I32) as ki_t, \
            nc.sbuf_tensor("rr", [128, NT, 16], F32) as rr_t, nc.sbuf_tensor("fx", [128, NT, 16], F32) as fx_t:
        posi, posf, invf, ang, kf, ki, rr, fx = [Buf(t, n) for t, n in (
            (posi_t, "posi"), (posf_t, "posf"), (invf_t, "invf"), (ang_t, "ang"), (kf_t, "kf"), (ki_t, "ki"), (rr_t, "rr"), (fx_t, "fx"))]
        dma("sp", l_c, posi[:], pos_d, writes=[posi])
        dma("sp", l_c, invf[:], invf_d, writes=[invf])
        op("dve", lambda e: e.tensor_copy(out=posf[:], in_=posi[:]), [posi], [posf])
        for t in range(NT):
            op("dve", lambda e, t=t: e.tensor_scalar(out=ang[:, t, :], in0=invf[:], scalar1=posf[:, t:t + 1], scalar2=None, op0=ALU.mult),
               [invf, posf], [ang])
        TWO_PI = 2.0 * math.pi
        C1 = 6.28125
        C2 = TWO_PI - C1
        for (dst, shift) in ((sinT, 0.0), (cosT, math.pi / 2)):
            op("dve", lambda e, shift=shift: e.tensor_scalar(out=rr[:], in0=ang[:], scalar1=shift, scalar2=None, op0=ALU.add), [ang], [rr])
            op("dve", lambda e: e.tensor_scalar(out=kf[:], in0=rr[:], scalar1=1.0 / TWO_PI, scalar2=None, op0=ALU.mult), [rr], [kf])
            op("dve", lambda e: e.tensor_copy(out=ki[:], in_=kf[:]), [kf], [ki])
            op("dve", lambda e: e.tensor_copy(out=kf[:], in_=ki[:]), [ki], [kf])
            op("dve", lambda e: e.scalar_tensor_tensor(out=rr[:], in0=kf[:], scalar=-C1, in1=rr[:], op0=ALU.mult, op1=ALU.add), [kf, rr], [rr])
            op("dve", lambda e: e.scalar_tensor_tensor(out=rr[:], in0=kf[:], scalar=-C2, in1=rr[:], op0=ALU.mult, op1=ALU.add), [kf, rr], [rr])
            op("dve", lambda e: e.tensor_scalar(out=fx[:], in0=rr[:], scalar1=math.pi, scalar2=-TWO_PI, op0=ALU.is_gt, op1=ALU.mult), [rr], [fx])
            op("dve", lambda e: e.tensor_tensor(out=rr[:], in0=rr[:], in1=fx[:], op=ALU.add), [rr, fx], [rr])
            op("dve", lambda e: e.tensor_scalar(out=fx[:], in0=rr[:], scalar1=-math.pi, scalar2=TWO_PI, op0=ALU.is_lt, op1=ALU.mult), [rr], [fx])
            op("dve", lambda e: e.tensor_tensor(out=rr[:], in0=rr[:], in1=fx[:], op=ALU.add), [rr, fx], [rr])
            op("dve", lambda e: e.tensor_scalar(out=rr[:], in0=rr[:], scalar1=math.pi, scalar2=-math.pi, op0=ALU.min, op1=ALU.max), [rr], [rr])
            op("act", lambda e, dst=dst: e.activation(out=dst[:], in_=rr[:], func=AF.Sin), [rr], [dst])
        op("dve", lambda e: e.tensor_copy(out=cosTd[:], in_=cosT[:, :, 0:16:2]), [cosT], [cosTd])
        op("dve", lambda e: e.tensor_copy(out=sinTd[:], in_=sinT[:, :, 0:16:2]), [sinT], [sinTd])
        Sx.barrier()

    CW = 4096
    WDIM = {n: (K, M) for n, K, M in WEIGHTS}
    PRO_W = ["ffn1_w_gate", "ffn1_w_up", "ffn1_w_down", "w_in", "mla_w_uq", "mla_w_ukv", "mem_w_kv"]
    BG_W = [n for n, _, _ in WEIGHTS if n not in PRO_W]

    def cast_pieces(names):
        out = []
        for n in names:
            K, M = WDIM[n]
            for r0 in range(0, K, 128):
                rows = min(128, K - r0)
                for c0 in range(0, M, CW):
                    out.append((n, r0, rows, c0, min(CW, M - c0)))
        return out

    class Caster:
        def __init__(self, ctx, tag, nslots, engs):
            self.f = [Buf(ctx.enter_context(nc.sbuf_tensor("cst_f%s%d" % (tag, i), [128, CW], F32)), "cf%s%d" % (tag, i)) for i in range(nslots)]
            self.b = [Buf(ctx.enter_context(nc.sbuf_tensor("cst_b%s%d" % (tag, i), [128, CW], BF16)), "cb%s%d" % (tag, i)) for i in range(nslots)]
            self.lin = [Sx.lane("l_cin%s%d" % (tag, i)) for i in range(nslots)]
            self.lout = [Sx.lane("l_cout%s%d" % (tag, i)) for i in range(nslots)]
            self.engs = engs
            self.i = 0

        def emit(self, piece):
            n, r0, rows, c0, cw = piece
            s_ = self.i % len(self.f)
            self.i += 1
            f, b = self.f[s_], self.b[s_]
            dma("sp", self.lin[s_], f[0:rows, 0:cw], w_d[n][r0:r0 + rows, c0:c0 + cw], writes=[f])
            eng = self.engs[s_ % len(self.engs)]
            if eng == "act":
                op("act", lambda e: e.activation(out=b[0:rows, 0:cw], in_=f[0:rows, 0:cw], func=AF.Copy), [f], [b])
            else:
                op(eng, lambda e: e.tensor_copy(out=b[0:rows, 0:cw], in_=f[0:rows, 0:cw]), [f], [b])
            dma("pool", self.lout[s_], wb[n][r0:r0 + rows, c0:c0 + cw], b[0:rows, 0:cw], reads=[b], writes=[wb_r[n]])

    with contextlib.ExitStack() as pcast:
        cst = Caster(pcast, "p", 4, ["dve", "act"])
        for piece in cast_pieces(PRO_W):
            cst.emit(piece)
        Sx.barrier()

    ss_pool = Pool([sb("ss%d" % i, [128, 1], F32) for i in range(4)])
    sq_junk = sb("sq_junk", [128, 1024], BF16)
    hb_pool = Pool([sb("hb%d" % i, [128, 1024], BF16) for i in range(2)])

    def rstd_of(src_ap, src_bufs, Dn):
        ss = ss_pool.get()
        inv = 1.0 / math.sqrt(Dn)
        op("act", lambda e: e.activation(out=sq_junk[:, 0:Dn], in_=src_ap, func=AF.Square, scale=inv, accum_out=ss[:]), src_bufs, [sq_junk, ss])
        op("dve", lambda e: e.tensor_scalar(out=ss[:], in0=ss[:], scalar1=EPS, scalar2=None, op0=ALU.add), [ss], [ss])
        op("act", lambda e: e.activation(out=ss[:], in_=ss[:], func=AF.Sqrt), [ss], [ss])
        op("dve", lambda e: e.reciprocal(out=ss[:], in_=ss[:]), [ss], [ss])
        return ss

    def rmsnorm_T(src_ap, src_bufs, Dn, gain, dst, dst_ap_fn):
        ss = rstd_of(src_ap, src_bufs, Dn)
        hb = hb_pool.get()
        op("dve", lambda e: e.scalar_tensor_tensor(out=hb[:, 0:Dn], in0=src_ap, scalar=ss[:, 0:1], in1=gain[:, 0:Dn], op0=ALU.mult, op1=ALU.mult),
           list(src_bufs) + [ss, gain], [hb])
        nk = Dn // 128
        pt = PS["T"].get()
        for kc in range(nk):
            op("pe", lambda e, kc=kc: e.transpose(out=pt[:, kc, :], in_=hb[:, kc * 128:(kc + 1) * 128], identity=identb[:]), [hb, identb], [pt])
        dst_ap = dst_ap_fn(nk)
        op("dve", lambda e: e.tensor_copy(out=dst_ap, in_=pt[:, 0:nk, :]), [pt], [dst])

    lw = [Sx.lane("l_w%d" % i) for i in range(4)]
    wctr = [0]
    FB = {}

    def alloc_ffn(ctx, tag, nslots, gcols):
        def a(name, shape, dt):
            return Buf(ctx.enter_context(nc.sbuf_tensor(name + tag, shape, dt)), name + tag)
        FB["wslots"] = [a("wslot%d" % i, [128, 8, gcols], BF16) for i in range(nslots)]
        FB["gcols"] = gcols
        FB["xb"] = a("xb", [128, 4, 1024], F32)
        FB["hT"] = a("hT", [128, 8, 512], BF16)
        FB["aT"] = a("aT", [128, 22, 512], BF16)
        FB["wd"] = a("wd", [128, 22, 1024], BF16)
        FB["sg"] = Pool([a("sg%d" % i, [128, 512], F32) for i in range(2)])

    def load_w512(name, K, c0, ncols):
        ns = len(FB["wslots"])
        s = wctr[0] % ns
        wctr[0] += 1
        slot = FB["wslots"][s]
        nk = K // 128
        dma("sp", lw[s], slot[:, 0:nk, 0:ncols], wb[name][:, c0:c0 + ncols].rearrange("(kc p) n -> p kc n", p=128),
            reads=[wb_r[name]], writes=[slot])
        return slot

    l_x = Sx.lane("l_x")
    l_x1 = Sx.lane("l_x1")
    l_wd = Sx.lane("l_wd")

    def ffn(pref, gain, xb=None):
        hT, aT, wd, sg_pool = FB["hT"], FB["aT"], FB["wd"], FB["sg"]
        if xb is None:
            xb = FB["xb"]
        G = FB["gcols"]
        for i in range(4):
            rmsnorm_T(xb[:, i, :], [xb], D, gain, hT, lambda nk, i=i: hT[:, 0:nk, i * 128:(i + 1) * 128])
        for c0 in range(0, DFF, G):
            ncols = min(G, DFF - c0)
            wg = load_w512(pref + "_w_gate", D, c0, ncols)
            wu = load_w512(pref + "_w_up", D, c0, ncols)
            for cc in range(ncols // 128):
                c = c0 // 128 + cc
                pg = PS["A"].get()
                for kc in range(8):
                    op("pe", lambda e, kc=kc, cc=cc, pg=pg, wg=wg: e.matmul(pg[:], lhsT=wg[:, kc, cc * 128:(cc + 1) * 128], rhs=hT[:, kc, :], start=(kc == 0), stop=(kc == 7)),
                       [wg, hT], [pg])
                pu = PS["A"].get()
                for kc in range(8):
                    op("pe", lambda e, kc=kc, cc=cc, pu=pu, wu=wu: e.matmul(pu[:], lhsT=wu[:, kc, cc * 128:(cc + 1) * 128], rhs=hT[:, kc, :], start=(kc == 0), stop=(kc == 7)),
                       [wu, hT], [pu])
                sg = sg_pool.get()
                op("act", lambda e, sg=sg, pg=pg: e.activation(out=sg[:], in_=pg[:], func=AF.Silu), [pg], [sg])
                op("dve", lambda e, sg=sg, pu=pu, c=c: e.tensor_tensor(out=aT[:, c, :], in0=sg[:], in1=pu[:], op=ALU.mult), [sg, pu], [aT])
        for half in range(2):
            for i in range(4):
                pd = PS["A"].get()
                for c in range(22):
                    op("pe", lambda e, c=c, i=i, half=half, pd=pd: e.matmul(pd[:], lhsT=aT[:, c, i * 128:(i + 1) * 128], rhs=wd[:, c, half * 512:(half + 1) * 512], start=(c == 0), stop=(c == 21)),
                       [aT, wd], [pd])
                op("dve", lambda e, i=i, half=half, pd=pd: e.scalar_tensor_tensor(out=xb[:, i, half * 512:(half + 1) * 512], in0=pd[:], scalar=0.5, in1=xb[:, i, half * 512:(half + 1) * 512], op0=ALU.mult, op1=ALU.add),
                   [pd, xb], [xb])

    def load_wd(pref):
        wd = FB["wd"]
        dma("sp", l_wd, wd[:], wb[pref + "_w_down"].rearrange("(c p) n -> p c n", p=128), reads=[wb_r[pref + "_w_down"]], writes=[wd])

    with contextlib.ExitStack() as pa:
        def sba(name, shape, dt):
            return Buf(pa.enter_context(nc.sbuf_tensor(name, shape, dt)), name)

        print("SBUF remaining at phase A start", nc.sbuf_bytes_remaining)
        set_psum(pa, "a", A=3, W=1, T=2, F=1)
        load_gains(pa, ["ffn1_norm", "mix_norm", "mla_q_norm", "mla_kv_norm"])
        alloc_ffn(pa, "_a", 3, 512)
        print("SBUF remaining after ffn alloc", nc.sbuf_bytes_remaining)
        xb, hT = FB["xb"], FB["hT"]
        wuq = sba("wuq", [128, 3, 768], BF16)
        wukv = sba("wukv", [128, 1024], BF16)
        def make_bs(tag):
            return dict(
                q_tok=sba("q_tok" + tag, [128, 8, 96], BF16), k_tok=sba("k_tok" + tag, [128, 8, 96], BF16),
                t1=sba("t1" + tag, [128, 128], F32), t2=sba("t2" + tag, [128, 128], F32),
                xs=sba("xs" + tag, [128, 256], F32),
                qTt=sba("qTt" + tag, [128, 8, 128], BF16), kTt=sba("kTt" + tag, [128, 8, 128], BF16),
                vt=sba("vt" + tag, [128, 512], BF16),
                lq=Sx.lane("l_stq" + tag), lk=Sx.lane("l_stk" + tag), lv=Sx.lane("l_stv" + tag))

        BSETS = [make_bs("0"), make_bs("1")]
        gq_fm = sba("gq_fm", [128, 4], F32)
        l_gfm = Sx.lane("l_gfm")
        for kc in range(3):
            dma("sp", l_gfm, gq_fm[:, kc:kc + 1], g_d["mla_q_norm"][0:1, kc * 128:(kc + 1) * 128].rearrange("o p -> p o"), writes=[gq_fm])
        dma("sp", l_gfm, gq_fm[:, 3:4], g_d["mla_kv_norm"][0:1, 0:128].rearrange("o p -> p o"), writes=[gq_fm])
        cqTb = sba("cqTb", [128, 4, 512], BF16)
        sqb = cqTb
        rbc = FB["sg"].bufs
        wkr_buf = sba("wkr_buf", [128, 8, 32], BF16)
        l_wkr = Sx.lane("l_wkr")
        dq_tok = sba("dq_tok", [128, 12, 64], BF16)
        dk_tok = sba("dk_tok", [128, 12, 64], BF16)
        dqTb = sba("dqTb", [128, 6, 512], BF16)
        dkTb = sba("dkTb", [128, 6, 512], BF16)
        dvb = sba("dvb", [128, 768], BF16)
        fst_pool = Pool([sba("fst%d" % i, [128, 2, 512], BF16) for i in range(2)])
        fm_slots = [sba("fmslot%d" % i, [128, 8, 256], BF16) for i in range(2)]
        l_fm = [Sx.lane("l_fm%d" % i) for i in range(2)]
        fm_ctr = [0]
        fm_banks = Pool(PS["F"].bufs + PS["A"].bufs)
        print("SBUF remaining after phase A alloc", nc.sbuf_bytes_remaining)

        def fm_gen(blk):
            tok0 = blk * 512
            for gi in range(14):
                c0 = 2848 + gi * 256
                k_ = fm_ctr[0] % 2
                fm_ctr[0] += 1
                wsl = fm_slots[k_]
                dma("sp", l_fm[k_], wsl[:], wb["w_in"][:, c0:c0 + 256].rearrange("(kc p) n -> p kc n", p=128), reads=[wb_r["w_in"]], writes=[wsl])
                fi = fst_pool.i % 2
                fst = fst_pool.get()
                for cc in range(2):
                    pf = fm_banks.get()
                    for kc in range(8):
                        op("pe", lambda e, kc=kc, cc=cc, pf=pf, wsl=wsl: e.matmul(pf[:], lhsT=wsl[:, kc, cc * 128:(cc + 1) * 128], rhs=hT[:, kc, :], start=(kc == 0), stop=(kc == 7)), [wsl, hT], [pf])
                    if gi < 2:
                        op("dve", lambda e, pf=pf, fst=fst, cc=cc: e.tensor_copy(out=fst[:, cc, :], in_=pf[:]), [pf], [fst])
                    else:
                        op("act", lambda e, pf=pf, fst=fst, cc=cc: e.activation(out=fst[:, cc, :], in_=pf[:], func=AF.Sigmoid), [pf], [fst])
                    if cc == 1:
                        if gi < 2:
                            dma("pool", l_fst[fi], mqTs[gi * 2:gi * 2 + 2, :, tok0:tok0 + 512].rearrange("h p t -> p h t"), fst[:], reads=[fst], writes=[R_mqTs])
                        else:
                            cch = (gi - 2) * 2
                            dma("pool", l_fst[fi], gTs[cch:cch + 2, :, tok0:tok0 + 512].rearrange("h p t -> p h t"), fst[:], reads=[fst], writes=[R_gTs])
                    yield
        l_a = Sx.lane("l_a_misc")
        l_st = {k: Sx.lane("l_st_" + k) for k in ("q", "k", "v", "dq", "dk", "dv")}
        l_fst = [Sx.lane("l_fst%d" % i) for i in range(2)]
        dma("sp", l_a, wuq[:], wb["mla_w_uq"].rearrange("(kc p) n -> p kc n", p=128), reads=[wb_r["mla_w_uq"]], writes=[wuq])
        dma("sp", l_a, wukv[:], wb["mla_w_ukv"], reads=[wb_r["mla_w_ukv"]], writes=[wukv])
        load_wd("ffn1")

        def rope(src3, nh, half, off, cs_t, dst3, t):
            cos_ap = cosT[:, t:t + 1, 0:16:(16 // half)].to_broadcast([128, nh, half])
            sin_ap = sinT[:, t:t + 1, 0:16:(16 // half)].to_broadcast([128, nh, half])
            return cos_ap, sin_ap


        def build_tables(t, bs):
            for (src, dM, dD) in ((cosT, bs["cosM"], bs["cosD"]), (sinT, bs["sinM"], bs["sinD"])):
                op("pool", lambda e, src=src, dM=dM: e.tensor_copy(out=dM[:, 0, :], in_=src[:, t, :]), [src], [dM])
                for (a0, n) in ((1, 1), (2, 2), (4, 4)):
                    op("pool", lambda e, dM=dM, a0=a0, n=n: e.tensor_copy(out=dM[:, a0:a0 + n, :], in_=dM[:, 0:n, :]), [dM], [dM])
                op("pool", lambda e, src=src, dD=dD: e.tensor_copy(out=dD[:, 0, :], in_=src[:, t, 0:16:2]), [src], [dD])
                for (a0, n) in ((1, 1), (2, 2), (4, 4), (8, 4)):
                    op("pool", lambda e, dD=dD, a0=a0, n=n: e.tensor_copy(out=dD[:, a0:a0 + n, :], in_=dD[:, 0:n, :]), [dD], [dD])

        def rope_apply(x_view, x_bufs, nh, half, t, dst_view, dst_buf, bs):
            xs, t1, t2 = bs["xs"], bs["t1"], bs["t2"]
            if half == 16:
                cb, sb_ = cosT, sinT
            else:
                cb, sb_ = cosTd, sinTd
            cos_ap = cb[:, t:t + 1, :].to_broadcast([128, nh, half])
            sin_ap = sb_[:, t:t + 1, :].to_broadcast([128, nh, half])
            xsv = xs[:, 0:nh * 2 * half].rearrange("p (h c) -> p h c", h=nh)
            op("act", lambda e: e.activation(out=xsv, in_=x_view(0, 2 * half), func=AF.Copy), list(x_bufs), [xs])
            x1 = xsv[:, :, 0:half]
            x2 = xsv[:, :, half:2 * half]
            a = t1[:, 0:nh * half].rearrange("p (h c) -> p h c", h=nh)
            b = t2[:, 0:nh * half].rearrange("p (h c) -> p h c", h=nh)
            op("dve", lambda e: e.tensor_tensor(out=a, in0=cos_ap, in1=x1, op=ALU.mult), [xs, cb], [t1])
            op("dve", lambda e: e.tensor_tensor(out=b, in0=sin_ap, in1=x2, op=ALU.mult), [xs, sb_], [t2])
            op("dve", lambda e: e.tensor_tensor(out=dst_view(0, half), in0=a, in1=b, op=ALU.subtract), [t1, t2], [dst_buf])
            op("dve", lambda e: e.tensor_tensor(out=a, in0=cos_ap, in1=x2, op=ALU.mult), [xs, cb], [t1])
            op("dve", lambda e: e.tensor_tensor(out=b, in0=sin_ap, in1=x1, op=ALU.mult), [xs, sb_], [t2])
            op("dve", lambda e: e.tensor_tensor(out=dst_view(half, 2 * half), in0=a, in1=b, op=ALU.add), [t1, t2], [dst_buf])

        def norm1(src_ap, src_bufs, Dn, gain, hb):
            ss = rstd_of(src_ap, src_bufs, Dn)
            op("dve", lambda e: e.scalar_tensor_tensor(out=hb[:, 0:Dn], in0=src_ap, scalar=ss[:, 0:1], in1=gain[:, 0:Dn], op0=ALU.mult, op1=ALU.mult),
               list(src_bufs) + [ss, gain], [hb])

        def norm2(hb, Dn, dst):
            nk = Dn // 128
            pt = PS["T"].get()
            for kc in range(nk):
                op("pe", lambda e, kc=kc, pt=pt: e.transpose(out=pt[:, kc, :], in_=hb[:, kc * 128:(kc + 1) * 128], identity=identb[:]), [hb, identb], [pt])
            op("dve", lambda e, pt=pt: e.tensor_copy(out=dst[:, 0:nk, :], in_=pt[:, 0:nk, :]), [pt], [dst])

        ga_banks = Pool(PS["F"].bufs + PS["A"].bufs)

        def ga_tile(blk, i, wA):
            bs = BSETS[i % 2]
            t = blk * 4 + i
            tokA = blk * 512 + i * 128
            q_tok, k_tok = bs["q_tok"], bs["k_tok"]
            pK = ga_banks.get()
            for kc in range(8):
                op("pe", lambda e, kc=kc, pK=pK: e.matmul(pK[:, 0:32], lhsT=hT[:, kc, i * 128:(i + 1) * 128], rhs=wkr_buf[:, kc, 0:32], start=(kc == 0), stop=(kc == 7)), [hT, wkr_buf], [pK])
            yield
            pQ = PS["W"].get()
            for (n0, n1) in ((0, 512), (512, 768)):
                for kc in range(3):
                    op("pe", lambda e, kc=kc, n0=n0, n1=n1, pQ=pQ: e.matmul(pQ[:, n0:n1], lhsT=cqTb[:, kc, i * 128:(i + 1) * 128], rhs=wuq[:, kc, n0:n1], start=(kc == 0), stop=(kc == 2)), [cqTb, wuq], [pQ])
            pQv = pQ[:, 0:768].rearrange("p (h c) -> p h c", h=8)
            op("act", lambda e: e.activation(out=q_tok[:, :, 0:64], in_=pQv[:, :, 0:64], func=AF.Copy), [pQ], [q_tok])
            rope_apply(lambda lo, hi: pQv[:, :, 64 + lo:64 + hi], [pQ], 8, 16, t, lambda lo, hi: q_tok[:, :, 64 + lo:64 + hi], q_tok, bs)
            pKV = PS["W"].get()
            for (n0, n1) in ((0, 512), (512, 1024)):
                op("pe", lambda e, n0=n0, n1=n1, pKV=pKV: e.matmul(pKV[:, n0:n1], lhsT=cqTb[:, 3, i * 128:(i + 1) * 128], rhs=wukv[:, n0:n1], start=True, stop=True), [cqTb, wukv], [pKV])
            pKVv = pKV[:].rearrange("p (h c) -> p h c", h=8)
            vt = bs["vt"]
            op("act", lambda e: e.activation(out=k_tok[:, :, 0:64], in_=pKVv[:, :, 0:64], func=AF.Copy), [pKV], [k_tok])
            op("act", lambda e: e.activation(out=vt[:].rearrange("p (h c) -> p h c", h=8), in_=pKVv[:, :, 64:128], func=AF.Copy), [pKV], [vt])
            pKv = pK[:, 0:32].rearrange("p (h c) -> p h c", h=1)
            rope_apply(lambda lo, hi: pKv[:, :, lo:hi], [pK], 1, 16, t, lambda lo, hi: k_tok[:, 0:1, 64 + lo:64 + hi], k_tok, bs)
            for (a0, n) in ((1, 1), (2, 2), (4, 4)):
                op("pool", lambda e, a0=a0, n=n: e.tensor_copy(out=k_tok[:, a0:a0 + n, 64:96], in_=k_tok[:, 0:n, 64:96]), [k_tok], [k_tok])
            dma("pool", bs["lv"], vs[tokA:tokA + 128, :, :].rearrange("p h c -> p (h c)"), vt[:], reads=[vt], writes=[R_vs])
            yield
            for (src, dstb, dram, lane, rres) in ((q_tok, bs["qTt"], qTs, bs["lq"], R_qTs), (k_tok, bs["kTt"], kTs, bs["lk"], R_kTs)):
                pt = PS["T"].get()
                for h in range(8):
                    op("pe", lambda e, h=h, src=src, pt=pt: e.transpose(out=pt[0:96, h, :], in_=src[:, h, :], identity=identb[:]), [src, identb], [pt])
                op("dve", lambda e, pt=pt, dstb=dstb: e.tensor_copy(out=dstb[0:96, :, :], in_=pt[0:96, :, :]), [pt], [dstb])
                dma("pool", lane, dram[:, :, tokA:tokA + 128].rearrange("h p t -> p h t"), dstb[0:96, :, :], reads=[dstb], writes=[rres])

        for blk in range(NB):
            tok0 = blk * 512
            dma("sp", l_x, xb[:], x_d[tok0:tok0 + 512, :].rearrange("(i p) d -> p i d", p=128), writes=[xb])
            ffn("ffn1", gbc["ffn1_norm"])
            Sx.mark("ffn1 done blk%d" % blk)
            dma("pool", l_x1, x1s[tok0:tok0 + 512, :].rearrange("(i p) d -> p i d", p=128), xb[:], reads=[xb], writes=[R_x1s])
            for i in range(4):
                rmsnorm_T(xb[:, i, :], [xb], D, gbc["mix_norm"], hT, lambda nk, i=i: hT[:, 0:nk, i * 128:(i + 1) * 128])
            wA = load_w512("w_in", D, 0, 512)
            dma("sp", l_wkr, wkr_buf[:], wb["w_in"][:, 512:544].rearrange("(kc p) n -> p kc n", p=128), reads=[wb_r["w_in"]], writes=[wkr_buf])
            pcs = []
            for cc in range(4):
                pc = ga_banks.get()
                pcs.append(pc)
                for kc in range(8):
                    op("pe", lambda e, kc=kc, cc=cc, pc=pc, wA=wA: e.matmul(pc[:], lhsT=wA[:, kc, cc * 128:(cc + 1) * 128], rhs=hT[:, kc, :], start=(kc == 0), stop=(kc == 7)), [wA, hT], [pc])
                op("act", lambda e, cc=cc, pc=pc: e.activation(out=sqb[:, cc, :], in_=pc[:], func=AF.Square), [pc], [sqb])
            pSS = PS["W"].get()
            for kc in range(3):
                op("pe", lambda e, kc=kc, pSS=pSS: e.matmul(pSS[:, 0:512], lhsT=onesb[:], rhs=sqb[:, kc, :], start=(kc == 0), stop=(kc == 2)), [onesb, sqb], [pSS])
            op("pe", lambda e, pSS=pSS: e.matmul(pSS[:, 512:1024], lhsT=onesb[:], rhs=sqb[:, 3, :], start=True, stop=True), [onesb, sqb], [pSS])
            for (j_, dn) in ((0, 384.0), (1, 128.0)):
                r_ = rbc[j_]
                op("dve", lambda e, j_=j_, dn=dn, r_=r_, pSS=pSS: e.tensor_scalar(out=r_[:], in0=pSS[:, j_ * 512:(j_ + 1) * 512], scalar1=1.0 / dn, scalar2=EPS, op0=ALU.mult, op1=ALU.add), [pSS], [r_])
                op("act", lambda e, r_=r_: e.activation(out=r_[:], in_=r_[:], func=AF.Sqrt), [r_], [r_])
                op("dve", lambda e, r_=r_: e.reciprocal(out=r_[:], in_=r_[:]), [r_], [r_])
            for cc in range(4):
                r_ = rbc[0] if cc < 3 else rbc[1]
                pc = pcs[cc]
                op("dve", lambda e, cc=cc, pc=pc, r_=r_: e.scalar_tensor_tensor(out=cqTb[:, cc, :], in0=pc[:], scalar=gq_fm[:, cc:cc + 1], in1=r_[:], op0=ALU.mult, op1=ALU.mult), [pc, gq_fm, r_], [cqTb])
            active = []
            nxt_tile = 0
            while active or nxt_tile < 4:
                if len(active) < 2 and nxt_tile < 4:
                    active.append(ga_tile(blk, nxt_tile, wA))
                    nxt_tile += 1
                for g_ in list(active):
                    try:
                        next(g_)
                    except StopIteration:
                        active.remove(g_)
            fm = fm_gen(blk)

            def fm_step():
                next(fm, None)

            for seg, base in (("dq", 544), ("dk", 1312), ("dv", 2080)):
                w0 = load_w512("w_in", D, base, 512)
                w1 = load_w512("w_in", D, base + 512, 256)
                for i in range(4):
                    t = blk * 4 + i
                    bsd = BSETS[i % 2]
                    pD = PS["W"].get()
                    for (wsl, n0, n1) in ((w0, 0, 512), (w1, 512, 768)):
                        for kc in range(8):
                            op("pe", lambda e, kc=kc, i=i, pD=pD, wsl=wsl, n0=n0, n1=n1: e.matmul(pD[:, n0:n1], lhsT=hT[:, kc, i * 128:(i + 1) * 128], rhs=wsl[:, kc, 0:n1 - n0], start=(kc == 0), stop=(kc == 7)), [hT, wsl], [pD])
                    pDv = pD[:, 0:768].rearrange("p (h c) -> p h c", h=12)
                    if seg == "dv":
                        op("act", lambda e, pD=pD, i=i: e.activation(out=dvb[:], in_=pD[:, 0:768], func=AF.Copy), [pD], [dvb])
                        dma("pool", l_st["dv"], dvs[tok0 + i * 128:tok0 + (i + 1) * 128, :, :].rearrange("p h c -> p (h c)"), dvb[:], reads=[dvb], writes=[R_dvs])
                        fm_step()
                        continue
                    dtok, dTb = (dq_tok, dqTb) if seg == "dq" else (dk_tok, dkTb)
                    op("act", lambda e, pDv=pDv, dtok=dtok: e.activation(out=dtok[:, :, 16:64], in_=pDv[:, :, 16:64], func=AF.Copy), [pD], [dtok])
                    rope_apply(lambda lo, hi, pDv=pDv: pDv[:, :, lo:hi], [pD], 12, 8, t, lambda lo, hi, dtok=dtok: dtok[:, :, lo:hi], dtok, bsd)
                    fm_step()
                    fm_step()
                    pt = PS["T"].get()
                    for j in range(6):
                        op("pe", lambda e, j=j, dtok=dtok, pt=pt: e.transpose(out=pt[:, j, :], in_=dtok[:, 2 * j:2 * j + 2, :].rearrange("p h c -> p (h c)"), identity=identb[:]), [dtok, identb], [pt])
                    op("dve", lambda e, pt=pt, dTb=dTb, i=i: e.tensor_copy(out=dTb[:, :, i * 128:(i + 1) * 128], in_=pt[:, 0:6, :]), [pt], [dTb])
                    fm_step()
            dma("pool", l_st["dq"], dqTs[:, :, tok0:tok0 + 512].rearrange("j p t -> p j t"), dqTb[:], reads=[dqTb], writes=[R_dqTs])
            dma("pool", l_st["dk"], dkTs[:, :, tok0:tok0 + 512].rearrange("j p t -> p j t"), dkTb[:], reads=[dkTb], writes=[R_dkTs])
            Sx.mark("dil done")
            for _ in fm:
                pass
        Sx.barrier()

    patt = contextlib.ExitStack()

    def sbt(name, shape, dt):
        return Buf(patt.enter_context(nc.sbuf_tensor(name, shape, dt)), name)

    pT_pool = Pool([sbt("pTs%d" % i, [128, 512], BF16) for i in range(3)])
    pTm_pool = Pool([sbt("pTm%d" % i, [128, 512], BF16) for i in range(2)])
    lrow = sbt("lrow", [128, 512], F32)
    rl = sbt("rl", [64, 512], F32)
    o_pool = Pool([sbt("o_st%d" % i, [64, 512], BF16) for i in range(2)])
    l_o = [Sx.lane("l_o%d" % i) for i in range(2)]

    lrow2 = Pool([lrow, sbt("lrow_b", [128, 512], F32)])
    bg_holder = {}
    bg_list = cast_pieces(BG_W)
    bg_pos = [0]

    def bg_emit(k):
        for _ in range(k):
            if bg_pos[0] < len(bg_list):
                bg_holder["c"].emit(bg_list[bg_pos[0]])
                bg_pos[0] += 1

    def finish_a(pO):
        lr = lrow2.get()
        op("act", lambda e: e.activation(out=lr[64:65, :], in_=pO[64:65, :], func=AF.Copy), [pO], [lr])
        return lr

    def finish_b(pO, lr, pB, dst_ap, dst_res):
        op("pe", lambda e: e.matmul(pB[0:64, :], lhsT=onesf[64:65, 0:64], rhs=lr[64:65, :], start=True, stop=True), [onesf, lr], [pB])
        op("dve", lambda e: e.reciprocal(out=rl[:], in_=pB[0:64, :]), [pB], [rl])
        oi = o_pool.i % 2
        ost = o_pool.get()
        op("dve", lambda e: e.tensor_tensor(out=ost[:], in0=rl[:], in1=pO[0:64, :], op=ALU.mult), [pO, rl], [ost])
        dma("pool", l_o[oi], dst_ap, ost[:], reads=[ost], writes=[dst_res])

    def run_stream(steps, look, defer):
        n = len(steps)
        pend = []
        for idx in range(n + look):
            if idx < n:
                st = steps[idx]
                for f_ in st.get("pre", ()):
                    f_()
                st["S"]()
            j = idx - look
            if j >= 0:
                steps[j]["PV"]()
                if steps[j].get("fin_a"):
                    pend.append([defer, steps[j]["fin_a"](), steps[j]["fin_b"]])
            for p in pend:
                p[0] -= 1
            while pend and pend[0][0] < 0:
                _, tok, fb = pend.pop(0)
                fb(tok)
        for _, tok, fb in pend:
            fb(tok)

    with contextlib.ExitStack() as pb:
        def sbb(name, shape, dt):
            return Buf(pb.enter_context(nc.sbuf_tensor(name, shape, dt)), name)

        bg_holder["c"] = Caster(pb, "g", 4, ["dve", "pool"])
        pS2 = Pool([psum(pb, "pS2b%d" % i, [128, 2, 512]) for i in range(2)])
        pOp = Pool([psum(pb, "pOb%d" % i, [128, 512]) for i in range(2)])
        pBp = Pool([psum(pb, "pBb%d" % i, [128, 512]) for i in range(1)])
        pT2 = Pool([sbb("pT2_%d" % i, [128, 2, 512], BF16) for i in range(3)])
        kTh = [sbb("kTh%d" % i, [128, S], BF16) for i in range(2)]
        qTh = [sbb("qTh%d" % i, [128, S], BF16) for i in range(2)]
        vh = [sbb("vh%d" % i, [128, NT, 65], BF16) for i in range(2)]
        l_b = [[Sx.lane("l_b%d_%d" % (i, j)) for j in range(3)] for i in range(2)]
        for i in range(2):
            op("pool", lambda e, i=i: e.memset(vh[i][:, :, 64:65], 1.0), [], [vh[i]])
        sc_mla = 96 ** -0.5

        def load_head(h):
            s = h % 2
            dma("sp", l_b[s][0], kTh[s][0:96, :], kTs[h], reads=[R_kTs], writes=[kTh[s]])
            dma("sp", l_b[s][1], qTh[s][0:96, :], qTs[h], reads=[R_qTs], writes=[qTh[s]])
            dma("sp", l_b[s][2], vh[s][:, :, 0:64], vs[:, h, :].rearrange("(t p) c -> p t c", p=128), reads=[R_vs], writes=[vh[s]])

        load_head(0)
        steps = []
        NP = NT // 2
        for h in range(8):
            s = h % 2
            for qb in range(NB):
                state = {}
                for p in range(NP):
                    st = {}

                    def S_(h=h, s=s, qb=qb, p=p, st=st, state=state):
                        if p == 0:
                            state["pO"] = pOp.get()
                        ps2 = pS2.get()
                        for j in range(2):
                            kt = 2 * p + j
                            op("pe", lambda e, kt=kt, j=j, ps2=ps2: e.matmul(ps2[:, j, :], lhsT=kTh[s][0:96, kt * 128:(kt + 1) * 128], rhs=qTh[s][0:96, qb * 512:(qb + 1) * 512], start=True, stop=True), [kTh[s], qTh[s]], [ps2])
                        pt = pT2.get()
                        op("act", lambda e, ps2=ps2, pt=pt: e.activation(out=pt[:].rearrange("p a b -> p (a b)"), in_=ps2[:].rearrange("p a b -> p (a b)"), func=AF.Exp, scale=sc_mla), [ps2], [pt])
                        st["pt"] = pt

                    def PV_(h=h, s=s, qb=qb, p=p, st=st, state=state):
                        pO = state["pO"]
                        pt = st["pt"]
                        for j in range(2):
                            kt = 2 * p + j
                            op("pe", lambda e, kt=kt, j=j, pt=pt, pO=pO: e.matmul(pO[0:65, :], lhsT=vh[s][:, kt, 0:65], rhs=pt[:, j, :], start=(kt == 0), stop=(kt == NT - 1)), [vh[s], pt], [pO])

                    st["S"] = S_
                    st["PV"] = PV_
                    st["pre"] = []
                    if p == 0 and qb == 1 and h + 1 < 8:
                        st["pre"].append(lambda h=h: load_head(h + 1))
                    if p == 0:
                        st["pre"].append(lambda: bg_emit(2))
                    if p == NP - 1:
                        st["fin_a"] = (lambda state=state: (state["pO"], finish_a(state["pO"])))
                        st["fin_b"] = (lambda tok, h=h, qb=qb: finish_b(tok[0], tok[1], pBp.get(), omlaTs[h, :, qb * 512:(qb + 1) * 512], R_omla))
                    steps.append(st)
        run_stream(steps, look=1, defer=2)
        bg_emit(len(bg_list))
        Sx.barrier()

    with contextlib.ExitStack() as pc:
        def sbc(name, shape, dt):
            return Buf(pc.enter_context(nc.sbuf_tensor(name, shape, dt)), name)

        L2 = S // 16
        KP = min(128, L2)
        NK2 = L2 // KP
        pSc = Pool([psum(pc, "pSc%d" % i, [128, 512]) for i in range(5)])
        pOc = Pool([psum(pc, "pOc%d" % i, [128, 512]) for i in range(2)])
        pBc = Pool([psum(pc, "pBc%d" % i, [128, 512]) for i in range(1)])
        pTc = Pool([sbc("pTc%d" % i, [128, 512], BF16) for i in range(6)])
        pMc = Pool([sbc("pMc%d" % i, [128, 512], BF16) for i in range(6)])
        NM01 = len([1 for (g_, _r) in mlist if g_ < 2])
        masks = sbc("masks_sb", [128, NM01, 512], BF16)
        masks2 = sbc("masks2_sb", [128, 2, 512], BF16)
        l_m = Sx.lane("l_masks")
        dma("sp", l_m, masks2[:], masks2_d.rearrange("m p q -> p m q"), writes=[masks2])
        dma("sp", l_m, masks[:], masks_d[0:NM01].rearrange("m p q -> p m q"), writes=[masks])
        mbias = sbc("mbias_sb", [128, NM01, 512], BF16)
        dma("sp", l_m, mbias[:], mbias_d[0:NM01].rearrange("m p q -> p m q"), writes=[mbias])
        dkT = [[sbc("dkT%d_%d" % (b, g), [128, S], BF16) for g in range(2)] for b in range(2)]
        dqT = [[sbc("dqT%d_%d" % (b, g), [128, S], BF16) for g in range(2)] for b in range(2)]
        dvh = [[sbc("dvh%d_%d" % (b, g), [128, NT, 65], BF16) for g in range(2)] for b in range(2)]
        dk2n = sbc("dk2n", [128, S], BF16)
        dq2n = sbc("dq2n", [128, S], BF16)
        dk2c = sbc("dk2c", [128, 16, L2], BF16)
        dq2c = sbc("dq2c", [128, 16, L2], BF16)
        v2c = sbc("v2c", [128, 16 * NK2, 65], BF16)
        acc2n = sbc("acc2n", [128, S], F32)
        sums = [sbc("sums%d" % i, [128, 512], F32) for i in range(2)]
        l_cd = [Sx.lane("l_cd%d" % i) for i in range(3)]
        l_c2 = [Sx.lane("l_c2_%d" % i) for i in range(3)]
        print("SBUF remaining in phase C", nc.sbuf_bytes_remaining)
        for b in range(2):
            for g in range(2):
                op("pool", lambda e, b=b, g=g: e.memset(dvh[b][g][:, :, 64:65], 1.0), [], [dvh[b][g]])
                z0 = 64 * (1 - b)
                op("pool", lambda e, b=b, g=g, z0=z0: e.memset(dqT[b][g][z0:z0 + 64, :], 0.0), [], [dqT[b][g]])
        op("pool", lambda e: e.memset(v2c[:, :, 64:65], 1.0), [], [v2c])
        sc_dil = 64 ** -0.5
        midx = {gr: i for i, gr in enumerate(mlist)}

        def load_j(j):
            b = j % 2
            for g in range(2):
                hd = g * 4 + j
                pair = hd // 2
                dma("sp", l_cd[0], dkT[b][g][:], dkTs[pair], reads=[R_dkTs], writes=[dkT[b][g]])
                rq = 64 * (hd % 2)
                dma("sp", l_cd[1], dqT[b][g][rq:rq + 64, :], dqTs[pair, rq:rq + 64, :], reads=[R_dqTs], writes=[dqT[b][g]])
                dma("sp", l_cd[2], dvh[b][g][:, :, 0:64], dvs[:, hd, :].rearrange("(t p) c -> p t c", p=128), reads=[R_dvs], writes=[dvh[b][g]])

        def load_qk2(j):
            hd = 8 + j
            pair = hd // 2
            rq = 64 * (hd % 2)
            zq = 64 - rq
            dma("sp", l_c2[0], dk2n[:], dkTs[pair], reads=[R_dkTs], writes=[dk2n])
            op("pool", lambda e: e.memset(dq2n[zq:zq + 64, :], 0.0), [], [dq2n])
            dma("sp", l_c2[1], dq2n[rq:rq + 64, :], dqTs[pair, rq:rq + 64, :], reads=[R_dqTs], writes=[dq2n])
            op("act", lambda e: e.activation(out=dk2c[:], in_=dk2n[:].rearrange("p (m r) -> p r m", r=16), func=AF.Copy), [dk2n], [dk2c])
            op("act", lambda e: e.activation(out=dq2c[:], in_=dq2n[:].rearrange("p (m r) -> p r m", r=16), func=AF.Copy), [dq2n], [dq2c])

        def load_v2(j):
            hd = 8 + j
            for r in range(16):
                pass
            dma("sp", l_c2[2], v2c[0:KP, :, 0:64].rearrange("p (r k) c -> p r k c", r=16),
                dvs[:, hd, :].rearrange("(k p r) c -> p r k c", p=KP, r=16), reads=[R_dvs], writes=[v2c])

        load_j(0)
        load_qk2(0)
        load_v2(0)
        steps = []
        for j in range(4):
            b = j % 2
            for r in range(16):
                state = {}
                for k2 in range(NK2):
                    st = {"pre": []}

                    def S_(r=r, k2=k2, st=st, state=state):
                        if k2 == 0:
                            state["pO"] = pOc.get()
                        pS = pSc.get()
                        op("pe", lambda e, pS=pS: e.matmul(pS[0:KP, 0:L2], lhsT=dk2c[:, r, k2 * KP:(k2 + 1) * KP], rhs=dq2c[:, r, :], start=True, stop=True), [dk2c, dq2c], [pS])
                        pt = pTc.get()
                        op("act", lambda e, pS=pS, pt=pt: e.activation(out=pt[0:KP, 0:L2], in_=pS[0:KP, 0:L2], func=AF.Exp, scale=sc_dil), [pS], [pt])
                        pm = pMc.get()
                        op("dve", lambda e, pt=pt, pm=pm: e.tensor_tensor(out=pm[0:KP, 0:L2], in0=pt[0:KP, 0:L2], in1=masks2[0:KP, k2, 0:L2], op=ALU.mult), [pt, masks2], [pm])
                        st["pm"] = pm

                    def PV_(r=r, k2=k2, st=st, state=state):
                        pO = state["pO"]
                        pm = st["pm"]
                        op("pe", lambda e, pm=pm, pO=pO: e.matmul(pO[0:65, 0:L2], lhsT=v2c[0:KP, r * NK2 + k2, 0:65], rhs=pm[0:KP, 0:L2], start=(k2 == 0), stop=(k2 == NK2 - 1)), [v2c, pm], [pO])

                    st["S"] = S_
                    st["PV"] = PV_
                    if k2 == NK2 - 1:
                        def fa(r=r, state=state):
                            pO = state["pO"]
                            op("act", lambda e: e.activation(out=acc2n[0:65, :].rearrange("p (m r) -> p r m", r=16)[:, r, :], in_=pO[0:65, 0:L2], func=AF.Copy), [pO], [acc2n])
                            return None
                        st["fin_a"] = fa
                        st["fin_b"] = (lambda tok: None)
                    steps.append(st)
            for qb in range(NB):
                q0 = qb * 512
                work = []
                for g in range(2):
                    for kt in range(NT):
                        rel = kt * 128 - q0
                        if (g, rel) in midx:
                            work.append((g, kt, midx[(g, rel)]))
                state = {}
                nw = len(work)
                for wi, (g, kt, mi) in enumerate(work):
                    st = {"pre": []}

                    def S_(b=b, g=g, kt=kt, mi=mi, q0=q0, wi=wi, st=st, state=state):
                        if wi == 0:
                            state["pO"] = pOc.get()
                        pS = pSc.get()
                        pe_mask = (wi % 2 == 1)
                        op("pe", lambda e, pS=pS: e.matmul(pS[:], lhsT=dkT[b][g][:, kt * 128:(kt + 1) * 128], rhs=dqT[b][g][:, q0:q0 + 512], start=True, stop=not pe_mask), [dkT[b][g], dqT[b][g]], [pS])
                        if pe_mask:
                            op("pe", lambda e, pS=pS: e.matmul(pS[:], lhsT=identb[:], rhs=mbias[:, mi, :], start=False, stop=True), [identb, mbias], [pS])
                        pt = pTc.get()
                        op("act", lambda e, pS=pS, pt=pt: e.activation(out=pt[:], in_=pS[:], func=AF.Exp, scale=sc_dil), [pS], [pt])
                        if pe_mask:
                            st["pm"] = pt
                        else:
                            pm = pMc.get()
                            op("dve", lambda e, pt=pt, pm=pm: e.tensor_tensor(out=pm[:], in0=pt[:], in1=masks[:, mi, :], op=ALU.mult), [pt, masks], [pm])
                            st["pm"] = pm

                    def PV_(b=b, g=g, kt=kt, wi=wi, nw=nw, st=st, state=state):
                        pO = state["pO"]
                        pm = st["pm"]
                        op("pe", lambda e, pm=pm, pO=pO: e.matmul(pO[0:65, :], lhsT=dvh[b][g][:, kt, 0:65], rhs=pm[:], start=(wi == 0), stop=(wi == nw - 1)), [dvh[b][g], pm], [pO])

                    st["S"] = S_
                    st["PV"] = PV_
                    if wi == 0 and qb == 0 and j + 1 < 4:
                        st["pre"].append(lambda j=j: load_qk2(j + 1))
                    if wi == 0 and qb == 1 and j + 1 < 4:
                        st["pre"].append(lambda j=j: load_j(j + 1))
                        st["pre"].append(lambda j=j: load_v2(j + 1))
                    if wi == nw - 1:
                        def fa(state=state, q0=q0, qb=qb, j=j):
                            pO = state["pO"]
                            sm = sums[(j * NB + qb) % 2]
                            op("dve", lambda e: e.tensor_tensor(out=sm[0:65, :], in0=acc2n[0:65, q0:q0 + 512], in1=pO[0:65, :], op=ALU.add), [acc2n, pO], [sm])
                            return sm

                        def fb(sm, j=j, q0=q0):
                            pB = pBc.get()
                            op("pe", lambda e: e.matmul(pB[0:64, :], lhsT=onesf[64:65, 0:64], rhs=sm[64:65, :], start=True, stop=True), [onesf, sm], [pB])
                            op("dve", lambda e: e.reciprocal(out=rl[:], in_=pB[0:64, :]), [pB], [rl])
                            oi = o_pool.i % 2
                            ost = o_pool.get()
                            op("dve", lambda e: e.tensor_tensor(out=ost[:], in0=rl[:], in1=sm[0:64, :], op=ALU.mult), [sm, rl], [ost])
                            dma("pool", l_o[oi], odilTs[j, :, q0:q0 + 512], ost[:], reads=[ost], writes=[R_odil])
                        st["fin_a"] = fa
                        st["fin_b"] = fb
                    steps.append(st)
        run_stream(steps, look=4, defer=5)
        Sx.barrier()

    with contextlib.ExitStack() as pd_:
        def sbd(name, shape, dt):
            return Buf(pd_.enter_context(nc.sbuf_tensor(name, shape, dt)), name)

        set_psum(pd_, "d", A=2, O=2, T=2, F=2)
        load_gains(pd_, ["mem_norm"])
        memx = sbd("memx", [128, 2, 1024], F32)
        memhT = sbd("memhT", [128, 8, 256], BF16)
        wkv = sbd("wkv", [128, 8, 1024], BF16)
        kmT = sbd("kmT", [128, 4, 256], BF16)
        vm = sbd("vm", [128, 2, 512], BF16)
        mqT = [sbd("mqT%d" % i, [128, S], BF16) for i in range(2)]
        rlm = sbd("rlm", [128, 512], F32)
        om_pool = Pool([sbd("om_st%d" % i, [128, 512], BF16) for i in range(2)])
        l_d = Sx.lane("l_d")
        l_dq = [Sx.lane("l_dq%d" % i) for i in range(2)]
        l_om = [Sx.lane("l_om%d" % i) for i in range(2)]
        dma("sp", l_d, memx[:], mem_d.rearrange("(i p) d -> p i d", p=128), writes=[memx])
        dma("sp", l_d, wkv[:], wb["mem_w_kv"].rearrange("(kc p) n -> p kc n", p=128), reads=[wb_r["mem_w_kv"]], writes=[wkv])
        for i in range(2):
            rmsnorm_T(memx[:, i, :], [memx], D, gbc["mem_norm"], memhT, lambda nk, i=i: memhT[:, 0:nk, i * 128:(i + 1) * 128])
        for h in range(4):
            pk = PS["A"].get()
            for kc in range(8):
                op("pe", lambda e, kc=kc, h=h, pk=pk: e.matmul(pk[:, 0:256], lhsT=wkv[:, kc, h * 128:(h + 1) * 128], rhs=memhT[:, kc, :], start=(kc == 0), stop=(kc == 7)), [wkv, memhT], [pk])
            op("dve", lambda e, h=h, pk=pk: e.tensor_copy(out=kmT[:, h, :], in_=pk[:, 0:256]), [pk], [kmT])
        for i in range(2):
            pv = PS["A"].get()
            for kc in range(8):
                op("pe", lambda e, kc=kc, i=i, pv=pv: e.matmul(pv[:], lhsT=memhT[:, kc, i * 128:(i + 1) * 128], rhs=wkv[:, kc, 512:1024], start=(kc == 0), stop=(kc == 7)), [memhT, wkv], [pv])
            op("dve", lambda e, i=i, pv=pv: e.tensor_copy(out=vm[:, i, :], in_=pv[:]), [pv], [vm])
        sc_mem = 128 ** -0.5
        for h in range(4):
            s = h % 2
            dma("sp", l_dq[s], mqT[s][:], mqTs[h], reads=[R_mqTs], writes=[mqT[s]])
            for qb in range(NB):
                pO = PS["O"].get()
                pL = PS["F"].get()
                pts = []
                for mt in range(2):
                    pS = PS["A"].get()
                    op("pe", lambda e, mt=mt, h=h, s=s, qb=qb, pS=pS: e.matmul(pS[:], lhsT=kmT[:, h, mt * 128:(mt + 1) * 128], rhs=mqT[s][:, qb * 512:(qb + 1) * 512], start=True, stop=True), [kmT, mqT[s]], [pS])
                    pt = pT_pool.get()
                    op("act", lambda e, pS=pS, pt=pt: e.activation(out=pt[:], in_=pS[:], func=AF.Exp, scale=sc_mem), [pS], [pt])
                    pts.append(pt)
                for mt in range(2):
                    op("pe", lambda e, mt=mt, h=h, pO=pO, pts=pts: e.matmul(pO[:], lhsT=vm[:, mt, h * 128:(h + 1) * 128], rhs=pts[mt][:], start=(mt == 0), stop=(mt == 1)), [vm, pts[mt]], [pO])
                for mt in range(2):
                    op("pe", lambda e, mt=mt, pL=pL, pts=pts: e.matmul(pL[:, 0:512], lhsT=onesb[:], rhs=pts[mt][:], start=(mt == 0), stop=(mt == 1)), [onesb, pts[mt]], [pL])
                op("dve", lambda e, pL=pL: e.reciprocal(out=rlm[:], in_=pL[:, 0:512]), [pL], [rlm])
                oi = om_pool.i % 2
                ost = om_pool.get()
                op("dve", lambda e, pO=pO, ost=ost: e.tensor_tensor(out=ost[:], in0=rlm[:], in1=pO[:], op=ALU.mult), [pO, rlm], [ost])
                dma("pool", l_om[oi], omemTs[h, :, qb * 512:(qb + 1) * 512], ost[:], reads=[ost], writes=[R_omem])
        Sx.barrier()

    patt.close()
    with contextlib.ExitStack() as pe_:
        def sbe(name, shape, dt):
            return Buf(pe_.enter_context(nc.sbuf_tensor(name, shape, dt)), name)

        print("SBUF remaining at phase E start", nc.sbuf_bytes_remaining)
        set_psum(pe_, "e", A=6, T=2)
        load_gains(pe_, ["ffn2_norm", "final_norm"])
        alloc_ffn(pe_, "_e", 3, 256)
        xbE, hTE = FB["xb"], FB["hT"]
        mixT = hTE
        womla = sbe("womla", [128, 4, 1024], BF16)
        wodil = sbe("wodil", [128, 2, 1024], BF16)
        womem = sbe("womem", [128, 4, 1024], BF16)
        wout = sbe("wout", [128, 8, 1024], BF16)
        omla = sbe("omla", [128, 4, 512], BF16)
        odil = sbe("odil", [128, 2, 512], BF16)
        omem = sbe("omem", [128, 4, 512], BF16)
        l_g3 = [Sx.lane("l_g3_%d" % i) for i in range(2)]
        m1 = sbe("m1", [128, 512], F32)
        m2 = sbe("m2", [128, 512], F32)
        l_e = Sx.lane("l_e")
        l_ein = [Sx.lane("l_ein%d" % i) for i in range(3)]
        l_out = Sx.lane("l_out")
        l_e2 = [l_e, Sx.lane("l_e_b"), Sx.lane("l_e_c"), Sx.lane("l_e_d")]
        dma("sp", l_e2[0], womla[:], wb["mla_w_o"].rearrange("(h p) n -> p h n", p=128), reads=[wb_r["mla_w_o"]], writes=[womla])
        dma("sp", l_e2[1], wodil[:], wb["dil_w_o"].rearrange("(h p) n -> p h n", p=128), reads=[wb_r["dil_w_o"]], writes=[wodil])
        dma("sp", l_e2[2], womem[:], wb["mem_w_o"].rearrange("(h p) n -> p h n", p=128), reads=[wb_r["mem_w_o"]], writes=[womem])
        print("SBUF remaining after phase E alloc", nc.sbuf_bytes_remaining)
        xbs = [xbE, sbe("xb2_e", [128, 4, 1024], F32)]
        l_x2 = [l_x, Sx.lane("l_x_b")]
        l_out2 = [l_out, Sx.lane("l_out_b")]
        g3s = [sbe("g3b_%d" % i, [128, 3, 512], BF16) for i in range(2)]
        m1s = [m1, sbe("m1b", [128, 512], F32)]
        m2s = [m2, sbe("m2b", [128, 512], F32)]
        m3s = [sbe("m3a", [128, 512], F32), sbe("m3b", [128, 512], F32)]
        omla_v = omlaTs.rearrange("h p t -> (h p) t").rearrange("(hp q) t -> q hp t", q=128)
        odil_v = odilTs.rearrange("h p t -> (h p) t").rearrange("(hp q) t -> q hp t", q=128)
        gts_v = gTs.rearrange("(b c) p t -> c p b t", b=3)

        def issue_loads(blk):
            t0 = blk * 512
            dma("sp", l_ein[0], omla[:], omla_v[:, :, t0:t0 + 512], reads=[R_omla], writes=[omla])
            dma("sp", l_ein[1], odil[:], odil_v[:, :, t0:t0 + 512], reads=[R_odil], writes=[odil])
            dma("sp", l_ein[2], omem[:], omemTs[:, :, t0:t0 + 512].rearrange("h p t -> p h t"), reads=[R_omem], writes=[omem])
            xb_ = xbs[blk % 2]
            dma("sp", l_x2[blk % 2], xb_[:], x1s[t0:t0 + 512, :].rearrange("(i p) d -> p i d", p=128), reads=[R_x1s], writes=[xb_])

        def load_g3(blk, c):
            k = (blk * 8 + c) % 2
            dma("sp", l_g3[k], g3s[k][:], gts_v[c][:, :, blk * 512:(blk + 1) * 512], reads=[R_gTs], writes=[g3s[k]])
            return g3s[k]

        issue_loads(0)
        nxt = load_g3(0, 0)
        dma("sp", l_e2[3], wout[:], wb["w_out"].rearrange("(h p) n -> p h n", p=128), reads=[wb_r["w_out"]], writes=[wout])
        load_wd("ffn2")
        for blk in range(NB):
            tok0 = blk * 512
            xb_ = xbs[blk % 2]
            for c in range(8):
                cs = slice(c * 128, (c + 1) * 128)
                gT = nxt
                if c < 7:
                    nxt = load_g3(blk, c + 1)
                elif blk + 1 < NB:
                    nxt = load_g3(blk + 1, 0)
                k = (blk * 8 + c) % 2
                m1_, m2_, m3_ = m1s[k], m2s[k], m3s[k]
                p1 = PS["A"].get()
                for h in range(4):
                    op("pe", lambda e, h=h, cs=cs, p1=p1: e.matmul(p1[:], lhsT=womla[:, h, cs], rhs=omla[:, h, :], start=(h == 0), stop=(h == 3)), [womla, omla], [p1])
                p2 = PS["A"].get()
                for h in range(2):
                    op("pe", lambda e, h=h, cs=cs, p2=p2: e.matmul(p2[:], lhsT=wodil[:, h, cs], rhs=odil[:, h, :], start=(h == 0), stop=(h == 1)), [wodil, odil], [p2])
                p3 = PS["A"].get()
                for h in range(4):
                    op("pe", lambda e, h=h, cs=cs, p3=p3: e.matmul(p3[:], lhsT=womem[:, h, cs], rhs=omem[:, h, :], start=(h == 0), stop=(h == 3)), [womem, omem], [p3])
                op("dve", lambda e, p1=p1, gT=gT, m1_=m1_: e.tensor_tensor(out=m1_[:], in0=gT[:, 0, :], in1=p1[:], op=ALU.mult), [p1, gT], [m1_])
                op("dve", lambda e, p2=p2, gT=gT, m2_=m2_: e.tensor_tensor(out=m2_[:], in0=gT[:, 1, :], in1=p2[:], op=ALU.mult), [p2, gT], [m2_])
                op("dve", lambda e, p3=p3, gT=gT, m3_=m3_: e.tensor_tensor(out=m3_[:], in0=gT[:, 2, :], in1=p3[:], op=ALU.mult), [p3, gT], [m3_])
                op("pool", lambda e, m1_=m1_, m2_=m2_: e.tensor_tensor(out=m1_[:], in0=m1_[:], in1=m2_[:], op=ALU.add), [m1_, m2_], [m1_])
                op("pool", lambda e, c=c, m1_=m1_, m3_=m3_: e.tensor_tensor(out=mixT[:, c, :], in0=m1_[:], in1=m3_[:], op=ALU.add), [m1_, m3_], [mixT])
            if blk + 1 < NB:
                issue_loads(blk + 1)
            for half in range(2):
                for i in range(4):
                    po = PS["A"].get()
                    for c in range(8):
                        op("pe", lambda e, c=c, i=i, half=half, po=po: e.matmul(po[:], lhsT=mixT[:, c, i * 128:(i + 1) * 128], rhs=wout[:, c, half * 512:(half + 1) * 512], start=(c == 0), stop=(c == 7)), [mixT, wout], [po])
                    op("dve", lambda e, i=i, half=half, po=po, xb_=xb_: e.tensor_tensor(out=xb_[:, i, half * 512:(half + 1) * 512], in0=xb_[:, i, half * 512:(half + 1) * 512], in1=po[:], op=ALU.add), [po, xb_], [xb_])
            ffn("ffn2", gbc["ffn2_norm"], xb_)
            gfin = gbc["final_norm"]
            for i in range(4):
                ss = rstd_of(xb_[:, i, :], [xb_], D)
                op("dve", lambda e, i=i, ss=ss, xb_=xb_: e.scalar_tensor_tensor(out=xb_[:, i, :], in0=xb_[:, i, :], scalar=ss[:, 0:1], in1=gfin[:], op0=ALU.mult, op1=ALU.mult), [xb_, ss, gfin], [xb_])
            dma("pool", l_out2[blk % 2], out_d[tok0:tok0 + 512, :].rearrange("(i p) d -> p i d", p=128), xb_[:], reads=[xb_], writes=[R_out])
        Sx.barrier()

    Sx.emit()
    stack.close()
    return nc


_CACHE = {}


def host_consts():
    if "c" not in _CACHE:
        _, masks = make_masks()
        half = 16
        invf = (500000.0 ** (-2.0 * np.arange(half, dtype=np.float32) / 32.0)).astype(np.float32)
        _CACHE["c"] = {
            "ident": np.eye(128, dtype=np.float32),
            "invf": np.ascontiguousarray(np.broadcast_to(invf[None, :], (128, 16))).astype(np.float32),
            "masks": masks,
            "mbias": ((masks.astype(np.float32) - 1.0) * 30000.0).astype(ml_dtypes.bfloat16),
        }
    return _CACHE["c"]


def make_masks2(S):
    L2 = S // 16
    KP = min(128, L2)
    m = np.zeros((2, 128, 512), np.float32)
    kk = np.arange(128)[:, None]
    qq = np.arange(512)[None, :]
    for k2 in range(2):
        m[k2] = (np.abs(k2 * KP + kk - qq) <= 64).astype(np.float32)
    return m.astype(ml_dtypes.bfloat16)


def make_in_map(b, S, inputs):
    c = host_consts()
    m = {
        "x": np.ascontiguousarray(inputs["x"][b]),
        "mem": np.ascontiguousarray(inputs["mem"][b]),
        "pos": np.ascontiguousarray(np.asarray(inputs["positions"][b]).astype(np.int32).reshape(S // 128, 128).T),
        "ident": c["ident"], "invf": c["invf"], "masks": c["masks"], "masks2": make_masks2(S), "mbias": c["mbias"],
    }
    for n, K, M in WEIGHTS:
        m[n] = np.ascontiguousarray(np.asarray(inputs[n], dtype=np.float32).reshape(K, M))
    for n, M in GAINS:
        m[n] = np.ascontiguousarray(np.asarray(inputs[n], dtype=np.float32).reshape(1, M))
    return m


def kernel(**inputs):
    inputs = {k: np.asarray(v) for k, v in inputs.items()}
    B, S, _ = inputs["x"].shape
    nc = build(S)
    in_maps = [make_in_map(b, S, inputs) for b in range(B)]
    res = run_bass_kernel_spmd(nc, in_maps, core_ids=list(range(B)))
    return np.stack([np.asarray(r["out"]) for r in res.results], axis=0).astype(np.float32)
```
